# Optimizing a Trainium2 kernel written in Bass

```python
import jax, jax.numpy as jnp
from jax import lax
import numpy as np

D_MODEL = 1024
BATCH = 1
SEQ = 16384
DEPTH = 2

MIX_WIDTH = D_MODEL
CONV_WIDTH = MIX_WIDTH // 2
ATT_WIDTH = MIX_WIDTH - CONV_WIDTH
HEAD_DIM = 64
N_ATT_HEADS = ATT_WIDTH // HEAD_DIM
N_CONV_GROUPS = CONV_WIDTH // HEAD_DIM
CONV_K = 3
N_MEM = 256
N_XHEADS = 4
XHEAD_DIM = D_MODEL // N_XHEADS
D_FF = 2816
Q_BLOCK = 128
EPS = 1e-6
IN_COLS = 3 * CONV_WIDTH + 3 * ATT_WIDTH + N_ATT_HEADS
SPLITS = [CONV_WIDTH, 2 * CONV_WIDTH, 3 * CONV_WIDTH,
          3 * CONV_WIDTH + ATT_WIDTH, 3 * CONV_WIDTH + 2 * ATT_WIDTH,
          3 * CONV_WIDTH + 3 * ATT_WIDTH]

kernel_name = "hybrid_conv_fox_macaron_memxattn"


def rmsnorm(x, g):
    xf = x.astype(jnp.float32)
    y = xf * lax.rsqrt(jnp.mean(xf * xf, axis=-1, keepdims=True) + EPS)
    return (y * g.astype(jnp.float32)).astype(x.dtype)


def swiglu(h, w_gu, w_down):
    gate, up = jnp.split(h @ w_gu, 2, axis=-1)
    return (jax.nn.silu(gate) * up) @ w_down


def causal_depthwise_conv(u, w):
    kern = w[:, None, :].astype(u.dtype)
    return lax.conv_general_dilated(
        u, kern, window_strides=(1,), padding=[(CONV_K - 1, 0)],
        dimension_numbers=('NWC', 'WIO', 'NWC'), feature_group_count=u.shape[-1])


def forgetting_attention(q, k, v, log_f):
    B, S, H, Dh = q.shape
    nb = S // Q_BLOCK
    scale = Dh ** -0.5
    c = jnp.cumsum(log_f, axis=1).transpose(0, 2, 1)
    qb = q.reshape(B, nb, Q_BLOCK, H, Dh).transpose(1, 0, 3, 2, 4)
    cqb = c.reshape(B, H, nb, Q_BLOCK).transpose(2, 0, 1, 3)
    kpos = jnp.arange(S)

    def block(args):
        i, q_i, cq_i = args
        s = jnp.einsum('bhqd,bkhd->bhqk', q_i, k,
                       preferred_element_type=jnp.float32) * scale
        s = s + cq_i[..., None] - c[:, :, None, :]
        qpos = i * Q_BLOCK + jnp.arange(Q_BLOCK)
        mask = kpos[None, :] <= qpos[:, None]
        s = jnp.where(mask, s, -jnp.inf)
        p = jax.nn.softmax(s, axis=-1)
        return jnp.einsum('bhqk,bkhd->bqhd', p.astype(v.dtype), v)

    out = lax.map(block, (jnp.arange(nb), qb, cqb))
    return out.transpose(1, 0, 2, 3, 4).reshape(B, S, H * Dh)


def memory_cross_attention(h, m, w_q, w_kv, w_o):
    B, S, _ = h.shape
    M = m.shape[1]
    q = (h @ w_q).reshape(B, S, N_XHEADS, XHEAD_DIM)
    k, v = jnp.split(m @ w_kv, 2, axis=-1)
    k = k.reshape(B, M, N_XHEADS, XHEAD_DIM)
    v = v.reshape(B, M, N_XHEADS, XHEAD_DIM)
    s = jnp.einsum('bshd,bmhd->bhsm', q, k,
                   preferred_element_type=jnp.float32) * (XHEAD_DIM ** -0.5)
    p = jax.nn.softmax(s, axis=-1)
    o = jnp.einsum('bhsm,bmhd->bshd', p.astype(v.dtype), v).reshape(B, S, D_MODEL)
    return o @ w_o


def setup_inputs(seed: int = 0) -> dict:
    key = jax.random.key(seed)
    ks = jax.random.split(key, 24)
    f32 = jnp.float32

    def w(k, shape, fan_in):
        return jax.random.normal(k, shape, f32) * (fan_in ** -0.5)

    def gain(k, shape):
        return 1.0 + 0.1 * jax.random.normal(k, shape, f32)

    return {
        "x": jax.random.normal(ks[0], (BATCH, SEQ, D_MODEL), f32),
        "mem": jax.random.normal(ks[1], (BATCH, N_MEM, D_MODEL), f32),
        "g_ffn1": gain(ks[2], (DEPTH, D_MODEL)),
        "w_ffn1_gu": w(ks[3], (DEPTH, D_MODEL, 2 * D_FF), D_MODEL),
        "w_ffn1_down": w(ks[4], (DEPTH, D_FF, D_MODEL), D_FF),
        "g_mix": gain(ks[5], (DEPTH, D_MODEL)),
        "w_mix_in": w(ks[6], (DEPTH, D_MODEL, IN_COLS), D_MODEL),
        "w_conv": w(ks[7], (DEPTH, CONV_K, CONV_WIDTH), CONV_K),
        "b_f": 2.0 + 0.5 * jax.random.normal(ks[8], (DEPTH, N_ATT_HEADS), f32),
        "g_conv_out": gain(ks[9], (DEPTH, CONV_WIDTH)),
        "g_att_out": gain(ks[10], (DEPTH, ATT_WIDTH)),
        "w_mix_out": w(ks[11], (DEPTH, MIX_WIDTH, D_MODEL), MIX_WIDTH),
        "g_xattn": gain(ks[12], (DEPTH, D_MODEL)),
        "g_mem": gain(ks[13], (DEPTH, D_MODEL)),
        "w_xq": w(ks[14], (DEPTH, D_MODEL, D_MODEL), D_MODEL),
        "w_xkv": w(ks[15], (DEPTH, D_MODEL, 2 * D_MODEL), D_MODEL),
        "w_xo": w(ks[16], (DEPTH, D_MODEL, D_MODEL), D_MODEL),
        "g_ffn2": gain(ks[17], (DEPTH, D_MODEL)),
        "w_ffn2_gu": w(ks[18], (DEPTH, D_MODEL, 2 * D_FF), D_MODEL),
        "w_ffn2_down": w(ks[19], (DEPTH, D_FF, D_MODEL), D_FF),
        "g_final": gain(ks[20], (D_MODEL,)),
    }


def reference(x, mem, g_ffn1, w_ffn1_gu, w_ffn1_down, g_mix, w_mix_in, w_conv, b_f,
              g_conv_out, g_att_out, w_mix_out, g_xattn, g_mem, w_xq, w_xkv, w_xo,
              g_ffn2, w_ffn2_gu, w_ffn2_down, g_final):
    B, S, _ = x.shape
    for l in range(DEPTH):
        x = x + 0.5 * swiglu(rmsnorm(x, g_ffn1[l]), w_ffn1_gu[l], w_ffn1_down[l])

        h = rmsnorm(x, g_mix[l])
        z = h @ w_mix_in[l]
        zb, zc, zv, zq, zk, zval, zf = jnp.split(z, SPLITS, axis=-1)

        y_conv = zb * causal_depthwise_conv(zc * zv, w_conv[l])

        log_f = jax.nn.log_sigmoid((zf + b_f[l]).astype(jnp.float32))
        q = zq.reshape(B, S, N_ATT_HEADS, HEAD_DIM)
        k = zk.reshape(B, S, N_ATT_HEADS, HEAD_DIM)
        v = zval.reshape(B, S, N_ATT_HEADS, HEAD_DIM)
        y_att = forgetting_attention(q, k, v, log_f)

        y = jnp.concatenate([rmsnorm(y_conv, g_conv_out[l]),
                             rmsnorm(y_att, g_att_out[l])], axis=-1)
        x = x + y @ w_mix_out[l]

        x = x + memory_cross_attention(rmsnorm(x, g_xattn[l]), rmsnorm(mem, g_mem[l]),
                                       w_xq[l], w_xkv[l], w_xo[l])

        x = x + 0.5 * swiglu(rmsnorm(x, g_ffn2[l]), w_ffn2_gu[l], w_ffn2_down[l])
    return rmsnorm(x, g_final)
```

```python
import os
from contextlib import ExitStack

import numpy as np
import ml_dtypes

import concourse.bass as bass
import concourse.mybir as mybir
from concourse.bass_utils import run_bass_kernel_spmd

F32 = mybir.dt.float32
BF16 = mybir.dt.bfloat16
AF = mybir.ActivationFunctionType
ALU = mybir.AluOpType

NCORES = 8
D = 1024
S = 16384
TOK = S // NCORES
NT = TOK // 512
KC = D // 128
DFF = 2816
NFF = DFF // 128
NH = 8
HD = 64
EPS = 1e-6
NMEM = 256
IN_COLS = 3080


class Prog:
    ENG = ("pe", "act", "dve", "pool", "sp")

    def __init__(self, nc):
        self.nc = nc
        self.ops = {e: [] for e in self.ENG}
        self.cnt = {e: 0 for e in self.ENG}
        self.waited = {e: {} for e in self.ENG}
        self.lastw = {}
        self.readers = {}
        self.dma_cnt = {}
        self.out_events = []
        self.pending = {}

    def _waits(self, eng, reads, writes):
        need = {}

        def add(ev, raw):
            if ev is None:
                return
            k, v = ev
            if k == eng:
                if eng == "pe" or not raw:
                    return
            if need.get(k, 0) < v:
                need[k] = v

        for b in reads:
            add(self.lastw.get(b), True)
        for b in writes:
            add(self.lastw.get(b), False)
            for k, v in self.readers.get(b, {}).items():
                add((k, v), False)
        w = []
        for k, v in need.items():
            if self.waited[eng].get(k, 0) < v:
                self.waited[eng][k] = v
                w.append((k, v))
        return w

    def _record(self, ev, reads, writes):
        k, v = ev
        for b in reads:
            r = self.readers.setdefault(b, {})
            if r.get(k, 0) < v:
                r[k] = v
        for b in writes:
            self.lastw[b] = ev
            self.readers[b] = {}

    def barrier(self):
        snap = dict(self.dma_cnt)
        for e in self.ENG:
            if self.cnt[e] > 0:
                snap[e] = self.cnt[e]
        for e in self.ENG:
            self.pending[e] = dict(snap)

    def _merge_pending(self, eng, w):
        pend = self.pending.get(eng)
        if pend:
            for k, v in pend.items():
                if k == eng:
                    continue
                if self.waited[eng].get(k, 0) < v:
                    self.waited[eng][k] = v
                    w = [x for x in w if x[0] != k] + [(k, v)]
            self.pending[eng] = None
        return w

    def op(self, eng, fn, reads=(), writes=()):
        w = self._merge_pending(eng, self._waits(eng, reads, writes))
        self.cnt[eng] += 1
        ev = (eng, self.cnt[eng])
        self.ops[eng].append((w, fn, (eng, 1)))
        self._record(ev, reads, writes)
        return ev

    def dma(self, q, out_ap, in_ap, key, reads=(), writes=(), is_output=False):
        w = self._merge_pending(q, self._waits(q, reads, writes))
        self.dma_cnt[key] = self.dma_cnt.get(key, 0) + 16
        ev = (key, self.dma_cnt[key])
        self.ops[q].append((w, lambda e: e.dma_start(out=out_ap, in_=in_ap), (key, 16)))
        self._record(ev, reads, writes)
        if is_output:
            self.out_events.append(ev)
        return ev

    def emit(self):
        nc = self.nc
        fin = {}
        for k, v in self.out_events:
            fin[k] = max(fin.get(k, 0), v)
        for e in self.ENG:
            if e != "sp" and self.cnt[e] > 0:
                fin[e] = self.cnt[e]
        finw = [(k, v) for k, v in fin.items()]
        keys = set(self.dma_cnt.keys()) | set(self.ENG)
        with ExitStack() as st:
            sems = {k: st.enter_context(nc.semaphore("s_" + str(k))) for k in sorted(keys)}
            block = st.enter_context(nc.Block())

            def run(e, name):
                for w, fn, inc in self.ops[name]:
                    for k, v in w:
                        e.wait_ge(sems[k], v)
                    ins = fn(e)
                    ins.then_inc(sems[inc[0]], inc[1])
                if name == "sp":
                    for k, v in finw:
                        e.wait_ge(sems[k], v)

            block.tensor(lambda e: run(e, "pe"))
            block.scalar(lambda e: run(e, "act"))
            block.vector(lambda e: run(e, "dve"))
            block.gpsimd(lambda e: run(e, "pool"))
            block.sync(lambda e: run(e, "sp"))


class Ctx:
    def __init__(self, nc, st):
        self.nc = nc
        self.st = st
        self.n = 0

    def sb(self, shape, dt, name=None):
        self.n += 1
        return self.st.enter_context(self.nc.sbuf_tensor(name or f"t{self.n}", list(shape), dt))

    def ps(self, shape, dt=F32, name=None):
        self.n += 1
        return self.st.enter_context(self.nc.psum_tensor(name or f"p{self.n}", list(shape), dt))


class Ring:
    def __init__(self, items):
        self.items = items
        self.i = 0

    def next(self):
        it = self.items[self.i % len(self.items)]
        self.i += 1
        return it


def mm_group(P, out_ap, pairs, reads, writes):
    def fn(e):
        ins = None
        n = len(pairs)
        for i, (l, r) in enumerate(pairs):
            ins = e.matmul(out_ap, lhsT=l, rhs=r, start=(i == 0), stop=(i == n - 1))
        return ins
    return P.op("pe", fn, reads=reads, writes=writes)


class TokCommon:
    def __init__(self, nc, P, C):
        self.nc, self.P, self.C = nc, P, C
        self.x = C.sb([128, KC, TOK], F32, "x_sb")
        self.h = C.sb([128, KC, TOK], BF16, "h_sb")
        self.ones = C.sb([128, 128], BF16, "ones_bf")
        self.sq = [C.sb([128, 512], BF16, f"sq{i}") for i in range(2)]
        self.rstd = C.sb([128, 512], F32, "rstd")
        self.gs = C.sb([128, 8 * KC], F32, "gs")
        self.ps_ss = C.ps([128, 512], F32, "ps_ss")
        self.sq_ring = Ring([0, 1])
        self.cst = C.sb([128, 2], F32, "cst")
        P.op("dve", lambda e: e.memset(self.ones[:], 1.0), writes=["ones"])
        P.op("dve", lambda e: e.memset(self.cst[:, 0:1], EPS), writes=["cst"])
        P.op("dve", lambda e: e.memset(self.cst[:, 1:2], 1.0), writes=["cst"])

    def load_gain(self, g_ap, slot, ncols=KC, scale=None):
        P = self.P
        dst = self.gs[:, slot * KC: slot * KC + ncols]
        P.dma("sp", dst, g_ap, key=f"ld_g{slot}", writes=[("gs", slot)])

    def rmsnorm(self, src, src_key, dst, dst_key, slot, nch=KC, ntok_tiles=NT, tw=512, dst_c0=0):
        P = self.P
        width = nch * 128
        for t in range(ntok_tiles):
            ts = slice(t * tw, (t + 1) * tw)
            for c in range(nch):
                i = self.sq_ring.next()
                sq = self.sq[i]
                P.op("act", lambda e, sq=sq, c=c, ts=ts: e.activation(out=sq[:, 0:tw], in_=src[:, c, ts], func=AF.Square),
                     reads=[(src_key, c, t)], writes=[("sq", i)])
                last = (c == nch - 1)
                P.op("pe", lambda e, sq=sq, c=c, last=last: e.matmul(
                    self.ps_ss[:, 0:tw], lhsT=self.ones[:], rhs=sq[:, 0:tw], start=(c == 0), stop=last),
                    reads=[("sq", i), "ones"], writes=["ps_ss"])
            P.op("act", lambda e: e.activation(out=self.rstd[:, 0:tw], in_=self.ps_ss[:, 0:tw], func=AF.Sqrt,
                                               bias=self.cst[:, 0:1], scale=1.0 / width),
                 reads=["ps_ss", "cst"], writes=["rstd"])
            P.op("dve", lambda e: e.reciprocal(self.rstd[:, 0:tw], self.rstd[:, 0:tw]), reads=["rstd"], writes=["rstd"])
            for c in range(nch):
                P.op("dve", lambda e, c=c, ts=ts: e.scalar_tensor_tensor(
                    out=dst[:, dst_c0 + c, ts], in0=src[:, c, ts], scalar=self.gs[:, slot * KC + c: slot * KC + c + 1],
                    in1=self.rstd[:, 0:tw], op0=ALU.mult, op1=ALU.mult),
                    reads=[(src_key, c, t), ("gs", slot), "rstd"], writes=[(dst_key, dst_c0 + c, t)])


class FFN:
    SG = 6

    def __init__(self, T, C):
        self.T = T
        P = T.P
        self.a = C.sb([128, self.SG, TOK], BF16, "a_sb")
        self.wgu = [C.sb([128, KC, 2, 256], BF16, f"wgu{i}") for i in range(3)]
        self.wd = [C.sb([128, D], BF16, f"wd{i}") for i in range(self.SG)]
        self.sil = [C.sb([128, 512], F32, f"sil{i}") for i in range(2)]
        self.psg = [C.ps([128, 512], F32, f"psg{i}") for i in range(2)]
        self.psu = [C.ps([128, 512], F32, f"psu{i}") for i in range(2)]
        self.psd = [C.ps([128, 512], F32, f"psd{i}") for i in range(2)]
        self.wgu_i = 0
        self.pair_i = 0
        self.psd_i = 0

    def run(self, wgu_ap, wd_ap):
        T, P = self.T, self.T.P
        wgu_v = wgu_ap.rearrange("(kc p) f -> p kc f", p=128)
        sgs = []
        j = 0
        while j < NFF:
            n = min(self.SG, NFF - j)
            sgs.append((j, n))
            j += n
        for (j0, n) in sgs:
            for jj in range(n):
                P.dma("pool", self.wd[jj][:], wd_ap[(j0 + jj) * 128:(j0 + jj + 1) * 128, :],
                      key=f"ld_wd{jj}", writes=[("wd", jj)])
            for lg in range(0, n, 2):
                wi = self.wgu_i % 3
                self.wgu_i += 1
                wt = self.wgu[wi]
                c0 = (j0 + lg) * 128
                P.dma("pool", wt[:, :, 0, :], wgu_v[:, :, c0:c0 + 256], key=f"ld_wg{wi}",
                      writes=[("wgu", wi, 0)])
                P.dma("pool", wt[:, :, 1, :], wgu_v[:, :, DFF + c0:DFF + c0 + 256], key=f"ld_wu{wi}",
                      writes=[("wgu", wi, 1)])
                for jl in range(2):
                    jj = lg + jl
                    for t in range(NT):
                        ts = slice(t * 512, (t + 1) * 512)
                        pi = self.pair_i % 2
                        self.pair_i += 1
                        psg, psu, sil = self.psg[pi], self.psu[pi], self.sil[pi]
                        mm_group(P, psg[:], [(wt[:, k, 0, jl * 128:(jl + 1) * 128], T.h[:, k, ts]) for k in range(KC)],
                                 reads=[("wgu", wi, 0)] + [("h", k, t) for k in range(KC)], writes=[("psg", pi)])
                        mm_group(P, psu[:], [(wt[:, k, 1, jl * 128:(jl + 1) * 128], T.h[:, k, ts]) for k in range(KC)],
                                 reads=[("wgu", wi, 1)] + [("h", k, t) for k in range(KC)], writes=[("psu", pi)])
                        P.op("act", lambda e, sil=sil, psg=psg: e.activation(out=sil[:], in_=psg[:], func=AF.Silu),
                             reads=[("psg", pi)], writes=[("sil", pi)])
                        P.op("dve", lambda e, sil=sil, psu=psu, jj=jj, ts=ts: e.tensor_tensor(
                            self.a[:, jj, ts], sil[:], psu[:], ALU.mult),
                            reads=[("sil", pi), ("psu", pi)], writes=[("a", jj, t)])
            for c in range(KC):
                for t in range(NT):
                    ts = slice(t * 512, (t + 1) * 512)
                    di = self.psd_i % 2
                    self.psd_i += 1
                    psd = self.psd[di]
                    mm_group(P, psd[:], [(self.wd[jj][:, c * 128:(c + 1) * 128], self.a[:, jj, ts]) for jj in range(n)],
                             reads=[("wd", jj) for jj in range(n)] + [("a", jj, t) for jj in range(n)],
                             writes=[("psd", di)])
                    P.op("dve", lambda e, psd=psd, c=c, ts=ts: e.scalar_tensor_tensor(
                        out=T.x[:, c, ts], in0=psd[:], scalar=0.5, in1=T.x[:, c, ts], op0=ALU.mult, op1=ALU.add),
                        reads=[("psd", di), ("x", c, t)], writes=[("x", c, t)])


def load_x(T, xT_ap):
    for c in range(KC):
        T.P.dma("sp", T.x[:, c, :], xT_ap[c * 128:(c + 1) * 128, :], key=f"ld_x{c}",
                writes=[("x", c, t) for t in range(NT)])


def store_x(T, out_ap, src=None, key="x"):
    src = T.x if src is None else src
    for c in range(KC):
        T.P.dma("sp", out_ap[c * 128:(c + 1) * 128, :], src[:, c, :], key=f"st_x{c}",
                reads=[(key, c, t) for t in range(NT)], is_output=True)


def build_A():
    nc = bass.Bass("TRN2", target_bir_lowering=False)
    dt = nc.dram_tensor
    xT = dt("xT", [D, TOK], F32, kind="ExternalInput").ap()
    g1 = dt("g1", [128, KC], F32, kind="ExternalInput").ap()
    wgu = dt("wgu", [D, 2 * DFF], F32, kind="ExternalInput").ap()
    wd = dt("wd", [DFF, D], F32, kind="ExternalInput").ap()
    gm = dt("gm", [128, KC], F32, kind="ExternalInput").ap()
    win = dt("win", [D, IN_COLS], F32, kind="ExternalInput").ap()
    bfn = dt("bf", [NH, 1], F32, kind="ExternalInput").ap()
    x1T = dt("x1T", [D, TOK], F32, kind="ExternalOutput").ap()
    uT = dt("uT", [512, TOK], F32, kind="ExternalOutput").ap()
    zbT = dt("zbT", [512, TOK], F32, kind="ExternalOutput").ap()
    qT = dt("qT", [512, TOK], BF16, kind="ExternalOutput").ap()
    kT = dt("kT", [512, TOK], BF16, kind="ExternalOutput").ap()
    vT = dt("vT", [512, TOK], BF16, kind="ExternalOutput").ap()
    lf = dt("lf", [NH, TOK], F32, kind="ExternalOutput").ap()

    P = Prog(nc)
    with ExitStack() as st:
        C = Ctx(nc, st)
        T = TokCommon(nc, P, C)
        ffn = FFN(T, C)
        load_x(T, xT)
        T.load_gain(g1, 0)
        T.load_gain(gm, 1)
        T.rmsnorm(T.x, "x", T.h, "h", 0)
        ffn.run(wgu, wd)
        store_x(T, x1T)
        T.rmsnorm(T.x, "x", T.h, "h", 1)
        mixin(T, C, ffn, win, bfn, uT, zbT, qT, kT, vT, lf)
        P.emit()
    return nc


def mixin(T, C, ffn, win, bfn, uT, zbT, qT, kT, vT, lf):
    P = T.P
    win_v = win.rearrange("(kc p) f -> p kc f", p=128)
    wz = [w[:].rearrange("p k a b -> p k (a b)") for w in ffn.wgu]
    wf = C.sb([128, KC, NH], BF16, "wf")
    stf = [C.sb([128, 512], F32, f"stf{i}") for i in range(4)]
    stb = [C.sb([128, 512], BF16, f"stb{i}") for i in range(4)]
    nb = C.sb([NH, 1], F32, "negb")
    lft = C.sb([NH, 512], F32, "lft")
    ex = C.sb([NH, 512], F32, "ex")
    stf_i = [0]
    stb_i = [0]
    wz_i = [ffn.wgu_i]
    pair_i = [0]

    def load_group(g):
        wi = wz_i[0] % 3
        wz_i[0] += 1
        P.dma("pool", wz[wi], win_v[:, :, g * 512:(g + 1) * 512], key=f"ld_wg{wi}",
              writes=[("wgu", wi, 0), ("wgu", wi, 1)])
        return wi

    def proj(wi, i, t, ps, pskey):
        ts = slice(t * 512, (t + 1) * 512)
        mm_group(P, ps[:], [(wz[wi][:, k, i * 128:(i + 1) * 128], T.h[:, k, ts]) for k in range(KC)],
                 reads=[("wgu", wi, 0), ("wgu", wi, 1)] + [("h", k, t) for k in range(KC)], writes=[pskey])

    def evac(ps, pskey, out_dram, i, t, dtype, scale=1.0):
        ts = slice(t * 512, (t + 1) * 512)
        if dtype == "f32":
            si = stf_i[0] % 4
            stf_i[0] += 1
            stt, sk = stf[si], ("stf", si)
        else:
            si = stb_i[0] % 4
            stb_i[0] += 1
            stt, sk = stb[si], ("stb", si)
        P.op("act", lambda e: e.activation(out=stt[:], in_=ps[:], func=AF.Copy, scale=float(scale)),
             reads=[pskey], writes=[sk])
        P.dma("sp", out_dram[i * 128:(i + 1) * 128, ts], stt[:], key=f"st_{sk[0]}{si}", reads=[sk], is_output=True)

    wi = load_group(0)
    for i in range(4):
        for t in range(NT):
            pi = pair_i[0] % 2
            pair_i[0] += 1
            proj(wi, i, t, ffn.psg[pi], ("psg", pi))
            evac(ffn.psg[pi], ("psg", pi), zbT, i, t, "f32")
    wc = load_group(1)
    wv = load_group(2)
    for i in range(4):
        for t in range(NT):
            ts = slice(t * 512, (t + 1) * 512)
            pi = pair_i[0] % 2
            pair_i[0] += 1
            psg, psu, sil = ffn.psg[pi], ffn.psu[pi], ffn.sil[pi]
            proj(wc, i, t, psg, ("psg", pi))
            proj(wv, i, t, psu, ("psu", pi))
            P.op("act", lambda e, sil=sil, psg=psg: e.activation(out=sil[:], in_=psg[:], func=AF.Copy),
                 reads=[("psg", pi)], writes=[("sil", pi)])
            si = stf_i[0] % 4
            stf_i[0] += 1
            stt = stf[si]
            P.op("dve", lambda e, stt=stt, sil=sil, psu=psu: e.tensor_tensor(stt[:], sil[:], psu[:], ALU.mult),
                 reads=[("sil", pi), ("psu", pi)], writes=[("stf", si)])
            P.dma("sp", uT[i * 128:(i + 1) * 128, ts], stt[:], key=f"st_stf{si}", reads=[("stf", si)], is_output=True)
    for g, dram, sc in ((3, qT, 0.125), (4, kT, 1.0), (5, vT, 1.0)):
        wi = load_group(g)
        for i in range(4):
            for t in range(NT):
                pi = pair_i[0] % 2
                pair_i[0] += 1
                proj(wi, i, t, ffn.psg[pi], ("psg", pi))
                evac(ffn.psg[pi], ("psg", pi), dram, i, t, "bf16", sc)
    P.dma("pool", wf[:], win_v[:, :, 3072:3080], key="ld_wf", writes=["wf"])
    P.dma("sp", nb[:], bfn, key="ld_bf", writes=["negb"])
    P.op("dve", lambda e: e.tensor_scalar(nb[:], nb[:], -1.0, None, ALU.mult), reads=["negb"], writes=["negb"])
    for t in range(NT):
        ts = slice(t * 512, (t + 1) * 512)
        pi = pair_i[0] % 2
        pair_i[0] += 1
        ps = ffn.psg[pi]
        mm_group(P, ps[0:NH, :], [(wf[:, k, :], T.h[:, k, ts]) for k in range(KC)],
                 reads=["wf"] + [("h", k, t) for k in range(KC)], writes=[("psg", pi)])
        P.op("act", lambda e, ps=ps: e.activation(out=ex[:], in_=ps[0:NH, :], func=AF.Exp, bias=nb[:], scale=-1.0),
             reads=[("psg", pi), "negb"], writes=["ex"])
        P.op("act", lambda e: e.activation(out=lft[:], in_=ex[:], func=AF.Ln, bias=T.cst[0:NH, 1:2], scale=1.0),
             reads=["ex", "cst"], writes=["lft"])
        P.op("dve", lambda e: e.tensor_scalar(lft[:], lft[:], -1.0, None, ALU.mult),
             reads=["lft"], writes=["lft"])
        P.dma("sp", lf[:, ts], lft[:], key="st_lf", reads=["lft"], is_output=True)


_CACHE = {}


def _prog(name, builder):
    if name not in _CACHE:
        _CACHE[name] = builder()
    return _CACHE[name]


def _gl(g):
    return np.ascontiguousarray(np.asarray(g, np.float32).reshape(-1, 128).T)


def run_A(xT_cores, g1, wgu, wd, gm, win, bf):
    nc = _prog("A", build_A)
    in_maps = []
    for c in range(NCORES):
        in_maps.append({
            "xT": xT_cores[c], "g1": _gl(g1), "wgu": wgu, "wd": wd, "gm": _gl(gm), "win": win,
            "bf": np.ascontiguousarray(np.asarray(bf, np.float32).reshape(NH, 1)),
        })
    res = run_bass_kernel_spmd(nc, in_maps, core_ids=list(range(NCORES)))
    return res.results


NBLK = S // 128
NCH = S // 512


def build_B():
    nc = bass.Bass("TRN2", target_bir_lowering=False)
    dt = nc.dram_tensor
    qTh = dt("qTh", [HD, S], BF16, kind="ExternalInput").ap()
    kTh = dt("kTh", [HD, S], BF16, kind="ExternalInput").ap()
    vh = dt("vh", [128, NBLK * HD], BF16, kind="ExternalInput").ap()
    lfh = dt("lfh", [NBLK, 128], F32, kind="ExternalInput").ap()
    cU = dt("cU", [128, 128], F32, kind="ExternalInput").ap()
    cW = dt("cW", [128, 128], F32, kind="ExternalInput").ap()
    cS = dt("cS", [128, NCH], F32, kind="ExternalInput").ap()
    cI = dt("cI", [128, 128], F32, kind="ExternalInput").ap()
    yT = dt("yT", [HD, S], F32, kind="ExternalOutput").ap()
    a_scr = dt("a_scr", [NBLK, 128], BF16).ap()

    P = Prog(nc)
    with ExitStack() as st:
        C = Ctx(nc, st)
        qa = C.sb([HD + 1, S], BF16, "qa")
        ka = C.sb([HD + 1, S], BF16, "ka")
        va = C.sb([128, NBLK, HD + 1], BF16, "va")
        vs = C.sb([128, NBLK * HD], BF16, "vs")
        lf = C.sb([128, 128], F32, "lf")
        lc = C.sb([128, 128], F32, "lc")
        cg = C.sb([128, 128], F32, "cg")
        cT = C.sb([128, 128], F32, "cT")
        abf = C.sb([128, 128], BF16, "abf")
        U = C.sb([128, 128], F32, "U")
        W = C.sb([128, 128], F32, "W")
        Sst = C.sb([128, NCH], F32, "Sst")
        I = C.sb([128, 128], F32, "I")
        Tb = C.sb([128, 128], F32, "Tb")
        onesf = C.sb([128, 128], F32, "onesf")
        cd = C.sb([128, 2], F32, "cd")
        crefs = C.sb([128, NCH], F32, "crefs")
        bias = [C.sb([128, NBLK], F32, f"bias{i}") for i in range(2)]
        pT = [C.sb([128, 512], BF16, f"pT{i}") for i in range(4)]
        rec = C.sb([HD + 1, 512], F32, "rec")
        bcs = C.sb([HD, 512], F32, "bcs")
        yo = [C.sb([HD, 512], F32, f"yo{i}") for i in range(2)]
        ps_s = [C.ps([128, 512], F32, f"ps_s{i}") for i in range(4)]
        ps_o = [C.ps([128, 512], F32, f"ps_o{i}") for i in range(2)]
        ps_b = C.ps([128, 512], F32, "ps_b")
        ps_m = C.ps([128, 512], F32, "ps_m")

        for i, (t, src) in enumerate(((lf, lfh), (U, cU), (W, cW), (Sst, cS), (I, cI))):
            P.dma("sp", t[:], src, key=f"ld_c{i}", writes=[t.name])
        nq = 4
        for i in range(nq):
            cs = slice(i * (S // nq), (i + 1) * (S // nq))
            P.dma("sp", ka[0:HD, cs], kTh[:, cs], key=f"ld_k{i}", writes=[("ka", i)])
            P.dma("sp", qa[0:HD, cs], qTh[:, cs], key=f"ld_q{i}", writes=[("qa", i)])
        P.op("dve", lambda e: e.memset(ka[HD:HD + 1, :], 1.0), writes=["ka_aug"])
        P.op("dve", lambda e: e.memset(onesf[:], 1.0), writes=["onesf"])
        P.op("pool", lambda e: e.memset(va[:], 1.0), writes=[("va", i) for i in range(4)])
        vs_v = vs[:].rearrange("p (b d) -> p b d", d=HD)
        for i in range(4):
            bs = slice(i * 32, (i + 1) * 32)
            P.dma("sp", vs[:, i * 32 * HD:(i + 1) * 32 * HD], vh[:, i * 32 * HD:(i + 1) * 32 * HD],
                  key=f"ld_v{i}", writes=[("vs", i)])
            P.op("pool", lambda e, bs=bs: e.tensor_copy(va[:, bs, 0:HD], vs_v[:, bs, :]),
                 reads=[("vs", i)], writes=[("va", i)])

        P.op("dve", lambda e: e.tensor_tensor_scan(lc[:], lf[:], lf[:], 0.0, ALU.add, ALU.bypass),
             reads=["lf"], writes=["lc"])
        P.op("dve", lambda e: e.tensor_scalar(Tb[:], onesf[:], lc[:, 127:128], None, ALU.mult),
             reads=["lc", "onesf"], writes=["Tb"])
        P.op("pe", lambda e: e.matmul(ps_m[:, 0:1], lhsT=U[:], rhs=lc[:, 127:128], start=True, stop=True),
             reads=["U", "lc"], writes=["ps_m0"])
        P.op("pe", lambda e: e.matmul(ps_m[:, 1:2], lhsT=W[:], rhs=lc[:, 127:128], start=True, stop=True),
             reads=["W", "lc"], writes=["ps_m1"])
        P.op("pe", lambda e: e.matmul(ps_m[:, 64:64 + NCH], lhsT=Tb[:], rhs=Sst[:], start=True, stop=True),
             reads=["Tb", "Sst"], writes=["ps_m2"])
        P.op("act", lambda e: e.activation(out=cd[:], in_=ps_m[:, 0:2], func=AF.Copy),
             reads=["ps_m0", "ps_m1"], writes=["cd"])
        P.op("act", lambda e: e.activation(out=crefs[:], in_=ps_m[:, 64:64 + NCH], func=AF.Copy),
             reads=["ps_m2"], writes=["crefs"])
        P.op("dve", lambda e: e.tensor_scalar(cg[:], lc[:], cd[:, 0:1], None, ALU.add),
             reads=["lc", "cd"], writes=["cg"])
        P.op("dve", lambda e: e.tensor_scalar(abf[:], lc[:], cd[:, 1:2], None, ALU.add),
             reads=["lc", "cd"], writes=["abf"])
        P.op("pe", lambda e: e.transpose(ps_b[:, 0:128], cg[:], I[:]), reads=["cg", "I"], writes=["ps_b"])
        P.op("act", lambda e: e.activation(out=cT[:], in_=ps_b[:, 0:128], func=AF.Copy),
             reads=["ps_b"], writes=["cT"])
        P.dma("sp", a_scr, abf[:], key="st_a", reads=["abf"], writes=["a_scr"])
        P.dma("sp", qa[HD:HD + 1, :], a_scr.rearrange("(o b) i -> o (b i)", o=1), key="ld_a",
              reads=["a_scr"], writes=["qa_aug"])

        s_i = 0
        for n in range(NCH):
            nj = 4 * n + 4
            bt = bias[n % 2]
            P.op("dve", lambda e, bt=bt, nj=nj, n=n: e.tensor_scalar(
                bt[:, 0:nj], cT[:, 0:nj], -1.0, crefs[:, n:n + 1], ALU.mult, ALU.add),
                reads=["cT", "crefs"], writes=[("bias", n % 2)])
            po = ps_o[n % 2]
            qi = (n * 512) // (S // 4)
            for j in range(nj):
                r = j - 4 * n
                q0 = max(0, r) * 128
                si = s_i % 4
                s_i += 1
                ps, pt = ps_s[si], pT[si]
                ki = (j * 128) // (S // 4)
                P.op("pe", lambda e, ps=ps, j=j, q0=q0, n=n: e.matmul(
                    ps[:, q0:512], lhsT=ka[:, j * 128:(j + 1) * 128], rhs=qa[:, n * 512 + q0:(n + 1) * 512],
                    start=True, stop=True),
                    reads=[("ka", ki), "ka_aug", ("qa", qi), "qa_aug"], writes=[("ps_s", si)])
                P.op("act", lambda e, ps=ps, pt=pt, bt=bt, j=j, q0=q0: e.activation(
                    out=pt[:, q0:512], in_=ps[:, q0:512], func=AF.Exp, bias=bt[:, j:j + 1], scale=1.0),
                    reads=[("ps_s", si), ("bias", n % 2)], writes=[("pT", si)])
                if r >= 0:
                    P.op("pool", lambda e, pt=pt, q0=q0: e.affine_select(
                        out=pt[:, q0:q0 + 128], in_=pt[:, q0:q0 + 128], pattern=[[1, 128]],
                        compare_op=ALU.is_ge, fill=0.0, base=0, channel_multiplier=-1),
                        reads=[("pT", si)], writes=[("pT", si)])
                P.op("pe", lambda e, po=po, pt=pt, j=j, q0=q0, nj=nj: e.matmul(
                    po[0:HD + 1, q0:512], lhsT=va[:, j, :], rhs=pt[:, q0:512],
                    start=(j == 0), stop=(j == nj - 1)),
                    reads=[("pT", si), ("va", j // 32)], writes=[("ps_o", n % 2)])
            P.op("dve", lambda e, po=po: e.reciprocal(rec[HD:HD + 1, :], po[HD:HD + 1, :]),
                 reads=[("ps_o", n % 2)], writes=["rec"])
            P.op("pe", lambda e: e.matmul(ps_b[0:HD, :], lhsT=onesf[HD:HD + 1, 0:HD], rhs=rec[HD:HD + 1, :],
                                          start=True, stop=True),
                 reads=["rec", "onesf"], writes=["ps_b"])
            P.op("act", lambda e: e.activation(out=bcs[:], in_=ps_b[0:HD, :], func=AF.Copy),
                 reads=["ps_b"], writes=["bcs"])
            yt = yo[n % 2]
            P.op("dve", lambda e, po=po, yt=yt: e.tensor_tensor(yt[:], po[0:HD, :], bcs[:], ALU.mult),
                 reads=[("ps_o", n % 2), "bcs"], writes=[("yo", n % 2)])
            P.dma("sp", yT[:, n * 512:(n + 1) * 512], yt[:], key=f"st_y{n % 2}", reads=[("yo", n % 2)],
                  is_output=True)
        P.emit()
    return nc


def b_consts():
    bp = np.arange(128)[:, None]
    bk = np.arange(128)[None, :]
    U = (bp < bk).astype(np.float32)
    W = ((bp < bk) & (bp // 4 == bk // 4)).astype(np.float32)
    Sst = (bp < 4 * np.arange(NCH)[None, :]).astype(np.float32)
    I = np.eye(128, dtype=np.float32)
    return {"cU": U, "cW": W, "cS": Sst, "cI": I}


def run_B(qT_heads, kT_heads, v_heads, lf_heads):
    nc = _prog("B", build_B)
    cst = b_consts()
    in_maps = []
    for h in range(NH):
        vh = np.ascontiguousarray(v_heads[h].reshape(NBLK, 128, HD).transpose(1, 0, 2).reshape(128, NBLK * HD))
        m = {"qTh": np.ascontiguousarray(qT_heads[h]), "kTh": np.ascontiguousarray(kT_heads[h]), "vh": vh,
             "lfh": np.ascontiguousarray(lf_heads[h].reshape(NBLK, 128))}
        m.update(cst)
        in_maps.append(m)
    res = run_bass_kernel_spmd(nc, in_maps, core_ids=list(range(NCORES)))
    return res.results


NXH = 4


def build_C(final):
    nc = bass.Bass("TRN2", target_bir_lowering=False)
    dt = nc.dram_tensor
    inp = lambda n, s, d=F32: dt(n, s, d, kind="ExternalInput").ap()
    x1T = inp("x1T", [D, TOK])
    uTh = inp("uTh", [512, TOK + 2])
    zbT = inp("zbT", [512, TOK])
    yaT = inp("yaT", [512, TOK])
    wcv = inp("wcv", [128, 12])
    gco = inp("gco", [128, 4])
    gao = inp("gao", [128, 4])
    wmo = inp("wmo", [D, D])
    gx = inp("gx", [128, KC])
    gme = inp("gme", [128, KC])
    memT = inp("memT", [D, NMEM])
    wxq = inp("wxq", [D, D])
    wxkv = inp("wxkv", [D, 2 * D])
    wxo = inp("wxo", [D, D])
    g2 = inp("g2", [128, KC])
    wgu = inp("wgu", [D, 2 * DFF])
    wd = inp("wd", [DFF, D])
    gf = inp("gf", [128, KC]) if final else None
    outT = dt("outT", [D, TOK], F32, kind="ExternalOutput").ap()

    P = Prog(nc)
    with ExitStack() as st:
        C = Ctx(nc, st)
        T = TokCommon(nc, P, C)
        wc = C.sb([128, 12], F32, "wc")
        load_x(T, x1T)
        P.dma("sp", wc[:], wcv, key="ld_wc", writes=["wc"])
        T.load_gain(gco, 0, ncols=4)
        T.load_gain(gao, 1, ncols=4)
        T.load_gain(gx, 2)
        T.load_gain(gme, 3)
        T.load_gain(g2, 4)
        if final:
            T.load_gain(gf, 5)

        with ExitStack() as s1:
            C1 = Ctx(nc, s1)
            C1.n = 100
            yc = C1.sb([128, 4, TOK], F32, "yc")
            ycat = C1.sb([128, KC, TOK], BF16, "ycat")
            ut = C1.sb([128, TOK + 2], F32, "ut")
            zt = C1.sb([128, TOK], F32, "zt")
            wb = C1.sb([128, KC, D], BF16, "wb0")
            psp = [C1.ps([128, 512], F32, f"psp{i}") for i in range(2)]
            P.dma("pool", wb[:], wmo.rearrange("(kc p) f -> p kc f", p=128), key="ld_wb0", writes=["wb0"])
            for ci in range(4):
                P.dma("sp", ut[:], uTh[ci * 128:(ci + 1) * 128, :], key="ld_ut", writes=["ut"])
                P.dma("sp", zt[:], zbT[ci * 128:(ci + 1) * 128, :], key="ld_zt", writes=["zt"])
                ykeys = [("yc", ci, t) for t in range(NT)]
                P.op("dve", lambda e, ci=ci: e.tensor_scalar(yc[:, ci, :], ut[:, 0:TOK], wc[:, ci * 3:ci * 3 + 1], None, ALU.mult),
                     reads=["ut", "wc"], writes=ykeys)
                for k in (1, 2):
                    P.op("dve", lambda e, ci=ci, k=k: e.scalar_tensor_tensor(
                        out=yc[:, ci, :], in0=ut[:, k:k + TOK], scalar=wc[:, ci * 3 + k:ci * 3 + k + 1],
                        in1=yc[:, ci, :], op0=ALU.mult, op1=ALU.add),
                        reads=["ut", "wc"] + ykeys, writes=ykeys)
                P.op("pool", lambda e, ci=ci: e.tensor_tensor(yc[:, ci, :], yc[:, ci, :], zt[:], ALU.mult),
                     reads=["zt"] + ykeys, writes=ykeys)
            T.rmsnorm(yc, "yc", ycat, "ycat", 0, nch=4, dst_c0=0)
            for ci in range(4):
                P.dma("sp", yc[:, ci, :], yaT[ci * 128:(ci + 1) * 128, :], key=f"ld_ya{ci}",
                      writes=[("yc", ci, t) for t in range(NT)])
            T.rmsnorm(yc, "yc", ycat, "ycat", 1, nch=4, dst_c0=4)
            pi = 0
            for c in range(KC):
                for t in range(NT):
                    ts = slice(t * 512, (t + 1) * 512)
                    ps = psp[pi % 2]
                    pk = ("psp", pi % 2)
                    pi += 1
                    mm_group(P, ps[:], [(wb[:, k, c * 128:(c + 1) * 128], ycat[:, k, ts]) for k in range(KC)],
                             reads=["wb0"] + [("ycat", k, t) for k in range(KC)], writes=[pk])
                    P.op("dve", lambda e, ps=ps, c=c, ts=ts: e.tensor_tensor(T.x[:, c, ts], ps[:], T.x[:, c, ts], ALU.add),
                         reads=[pk, ("x", c, t)], writes=[("x", c, t)])
        P.barrier()

        with ExitStack() as s2:
            C2 = Ctx(nc, s2)
            C2.n = 200
            wbs = [C2.sb([128, KC, D], BF16, f"wbx{i}") for i in range(2)]
            mt = C2.sb([128, KC, NMEM], F32, "mt")
            mn = C2.sb([128, KC, NMEM], BF16, "mn")
            kmT = C2.sb([128, KC, NMEM], BF16, "kmT")
            vm = C2.sb([128, 2, D], BF16, "vm")
            qx = C2.sb([128, KC, 512], BF16, "qx")
            oT = C2.sb([128, KC, 512], BF16, "oT")
            pm = [C2.sb([128, 512], BF16, f"pm{i}") for i in range(2)]
            rden = C2.sb([128, 512], F32, "rden")
            psp = [C2.ps([128, 512], F32, f"psq{i}") for i in range(2)]
            pss = [C2.ps([128, 512], F32, f"pss{i}") for i in range(2)]
            pso = [C2.ps([128, 512], F32, f"pso{i}") for i in range(2)]
            psd = C2.ps([128, 512], F32, "psden")
            wv = lambda w: w.rearrange("(kc p) f -> p kc f", p=128)
            P.dma("pool", wbs[0][:], wv(wxkv)[:, :, 0:D], key="ld_wbx0", writes=["wbx0"])
            P.dma("pool", wbs[1][:], wv(wxkv)[:, :, D:2 * D], key="ld_wbx1", writes=["wbx1"])
            P.dma("sp", mt[:], memT.rearrange("(kc p) m -> p kc m", p=128), key="ld_mt",
                  writes=[("mt", c, 0) for c in range(KC)])
            T.rmsnorm(mt, "mt", mn, "mn", 3, nch=KC, ntok_tiles=1, tw=NMEM)
            mnk = [("mn", c, 0) for c in range(KC)]
            pi = 0
            for fc in range(KC):
                ps, pk = psp[pi % 2], ("psq", pi % 2)
                pi += 1
                mm_group(P, ps[:, 0:NMEM], [(wbs[0][:, k, fc * 128:(fc + 1) * 128], mn[:, k, :]) for k in range(KC)],
                         reads=["wbx0"] + mnk, writes=[pk])
                P.op("act", lambda e, ps=ps, fc=fc: e.activation(out=kmT[:, fc, :], in_=ps[:, 0:NMEM], func=AF.Copy),
                     reads=[pk], writes=["kmT"])
            for mb in range(2):
                for fh in range(2):
                    ps, pk = psp[pi % 2], ("psq", pi % 2)
                    pi += 1
                    mm_group(P, ps[:], [(mn[:, k, mb * 128:(mb + 1) * 128], wbs[1][:, k, fh * 512:(fh + 1) * 512])
                                        for k in range(KC)], reads=["wbx1"] + mnk, writes=[pk])
                    P.op("act", lambda e, ps=ps, mb=mb, fh=fh: e.activation(
                        out=vm[:, mb, fh * 512:(fh + 1) * 512], in_=ps[:], func=AF.Copy), reads=[pk], writes=["vm"])
            P.dma("pool", wbs[0][:], wv(wxq), key="ld_wbx0", writes=["wbx0"])
            P.dma("pool", wbs[1][:], wv(wxo), key="ld_wbx1", writes=["wbx1"])
            T.rmsnorm(T.x, "x", T.h, "h", 2)
            si = 0
            oi = 0
            for t in range(NT):
                ts = slice(t * 512, (t + 1) * 512)
                for fc in range(KC):
                    ps, pk = psp[pi % 2], ("psq", pi % 2)
                    pi += 1
                    mm_group(P, ps[:], [(wbs[0][:, k, fc * 128:(fc + 1) * 128], T.h[:, k, ts]) for k in range(KC)],
                             reads=["wbx0"] + [("h", k, t) for k in range(KC)], writes=[pk])
                    P.op("act", lambda e, ps=ps, fc=fc: e.activation(out=qx[:, fc, :], in_=ps[:], func=AF.Copy),
                         reads=[pk], writes=[("qx", fc)])
                for hh in range(NXH):
                    for mb in range(2):
                        ps, pk = pss[si % 2], ("pss", si % 2)
                        pt, ptk = pm[mb], ("pm", mb)
                        si += 1
                        mm_group(P, ps[:], [(kmT[:, 2 * hh + dc, mb * 128:(mb + 1) * 128], qx[:, 2 * hh + dc, :])
                                            for dc in range(2)],
                                 reads=["kmT", ("qx", 2 * hh), ("qx", 2 * hh + 1)], writes=[pk])
                        P.op("act", lambda e, ps=ps, pt=pt: e.activation(out=pt[:], in_=ps[:], func=AF.Exp, scale=1.0 / 16.0),
                             reads=[pk], writes=[ptk])
                    mm_group(P, psd[:], [(T.ones[:], pm[mb][:]) for mb in range(2)],
                             reads=["ones", ("pm", 0), ("pm", 1)], writes=["psden"])
                    P.op("dve", lambda e: e.reciprocal(rden[:], psd[:]), reads=["psden"], writes=["rden"])
                    for dc in range(2):
                        fc = 2 * hh + dc
                        ps, pk = pso[oi % 2], ("pso", oi % 2)
                        oi += 1
                        mm_group(P, ps[:], [(vm[:, mb, fc * 128:(fc + 1) * 128], pm[mb][:]) for mb in range(2)],
                                 reads=["vm", ("pm", 0), ("pm", 1)], writes=[pk])
                        P.op("dve", lambda e, ps=ps, fc=fc: e.tensor_tensor(oT[:, fc, :], ps[:], rden[:], ALU.mult),
                             reads=[pk, "rden"], writes=[("oT", fc)])
                for c in range(KC):
                    ps, pk = psp[pi % 2], ("psq", pi % 2)
                    pi += 1
                    mm_group(P, ps[:], [(wbs[1][:, k, c * 128:(c + 1) * 128], oT[:, k, :]) for k in range(KC)],
                             reads=["wbx1"] + [("oT", k) for k in range(KC)], writes=[pk])
                    P.op("dve", lambda e, ps=ps, c=c, ts=ts: e.tensor_tensor(T.x[:, c, ts], ps[:], T.x[:, c, ts], ALU.add),
                         reads=[pk, ("x", c, t)], writes=[("x", c, t)])
        P.barrier()

        with ExitStack() as s3:
            C3 = Ctx(nc, s3)
            C3.n = 300
            ffn = FFN(T, C3)
            T.rmsnorm(T.x, "x", T.h, "h", 4)
            ffn.run(wgu, wd)
        if final:
            P.barrier()
            with ExitStack() as s4:
                C4 = Ctx(nc, s4)
                C4.n = 400
                of = C4.sb([128, KC, TOK], F32, "of")
                T.rmsnorm(T.x, "x", of, "of", 5)
                store_x(T, outT, src=of, key="of")
                P.emit()
        else:
            store_x(T, outT)
            P.emit()
    return nc


def run_C(final, x1T_cores, uTh_cores, zbT_cores, yaT_cores, w, l, memT):
    name = "Cf" if final else "C"
    nc = _prog(name, lambda: build_C(final))
    wcv = np.ascontiguousarray(np.asarray(w["w_conv"][l], np.float32).T.reshape(4, 128, 3).transpose(1, 0, 2).reshape(128, 12))
    in_maps = []
    for c in range(NCORES):
        m = {
            "x1T": x1T_cores[c], "uTh": uTh_cores[c], "zbT": zbT_cores[c], "yaT": yaT_cores[c],
            "wcv": wcv, "gco": _gl(w["g_conv_out"][l]), "gao": _gl(w["g_att_out"][l]), "wmo": w["w_mix_out"][l],
            "gx": _gl(w["g_xattn"][l]), "gme": _gl(w["g_mem"][l]), "memT": memT,
            "wxq": w["w_xq"][l], "wxkv": w["w_xkv"][l], "wxo": w["w_xo"][l],
            "g2": _gl(w["g_ffn2"][l]), "wgu": w["w_ffn2_gu"][l], "wd": w["w_ffn2_down"][l],
        }
        if final:
            m["gf"] = _gl(w["g_final"])
        in_maps.append(m)
    res = run_bass_kernel_spmd(nc, in_maps, core_ids=list(range(NCORES)))
    return res.results


def kernel(**inputs):
    w = {k: np.asarray(v) for k, v in inputs.items()}
    x = w["x"][0]
    memT = np.ascontiguousarray(w["mem"][0].T)
    xT = [np.ascontiguousarray(x[c * TOK:(c + 1) * TOK].T) for c in range(NCORES)]
    depth = w["g_ffn1"].shape[0]
    for l in range(depth):
        ra = run_A(xT, w["g_ffn1"][l], w["w_ffn1_gu"][l], w["w_ffn1_down"][l], w["g_mix"][l],
                   w["w_mix_in"][l], w["b_f"][l])
        cat = lambda k: np.concatenate([ra[c][k] for c in range(NCORES)], axis=1)
        qf, kf, vf, lff, uf = cat("qT"), cat("kT"), cat("vT"), cat("lf"), cat("uT")
        qh = [qf[h * HD:(h + 1) * HD] for h in range(NH)]
        kh = [kf[h * HD:(h + 1) * HD] for h in range(NH)]
        vh = [np.ascontiguousarray(vf[h * HD:(h + 1) * HD].T) for h in range(NH)]
        lh = [np.ascontiguousarray(lff[h]) for h in range(NH)]
        rb = run_B(qh, kh, vh, lh)
        ya = np.concatenate([rb[h]["yT"] for h in range(NH)], axis=0)
        up = np.concatenate([np.zeros((512, 2), np.float32), uf], axis=1)
        rc = run_C(l == depth - 1,
                   [ra[c]["x1T"] for c in range(NCORES)],
                   [np.ascontiguousarray(up[:, c * TOK:(c + 1) * TOK + 2]) for c in range(NCORES)],
                   [ra[c]["zbT"] for c in range(NCORES)],
                   [np.ascontiguousarray(ya[:, c * TOK:(c + 1) * TOK]) for c in range(NCORES)],
                   w, l, memT)
        xT = [rc[c]["outT"] for c in range(NCORES)]
    out = np.concatenate([np.asarray(xT[c], np.float32).T for c in range(NCORES)], axis=0)
    return np.ascontiguousarray(out[None]).astype(np.float32)
```

```python
import os
from contextlib import ExitStack

import numpy as np
import ml_dtypes

import concourse.bass as bass
import concourse.mybir as mybir
from concourse.bass_utils import run_bass_kernel_spmd

F32 = mybir.dt.float32
BF16 = mybir.dt.bfloat16
AF = mybir.ActivationFunctionType
ALU = mybir.AluOpType

NCORES = 8
D = 1024
S = 16384
TOK = S // NCORES
NT = TOK // 512
KC = D // 128
DFF = 2816
NFF = DFF // 128
NH = 8
HD = 64
EPS = 1e-6
NMEM = 256
IN_COLS = 3080


class Prog:
    ENG = ("pe", "act", "dve", "pool", "sp")

    def __init__(self, nc):
        self.nc = nc
        self.ops = {e: [] for e in self.ENG}
        self.cnt = {e: 0 for e in self.ENG}
        self.waited = {e: {} for e in self.ENG}
        self.lastw = {}
        self.readers = {}
        self.dma_cnt = {}
        self.out_events = []
        self.pending = {}

    def _waits(self, eng, reads, writes):
        need = {}

        def add(ev, raw):
            if ev is None:
                return
            k, v = ev
            if k == eng:
                if eng == "pe" or not raw:
                    return
            if need.get(k, 0) < v:
                need[k] = v

        for b in reads:
            add(self.lastw.get(b), True)
        for b in writes:
            add(self.lastw.get(b), False)
            for k, v in self.readers.get(b, {}).items():
                add((k, v), False)
        w = []
        for k, v in need.items():
            if self.waited[eng].get(k, 0) < v:
                self.waited[eng][k] = v
                w.append((k, v))
        return w

    def _record(self, ev, reads, writes):
        k, v = ev
        for b in reads:
            r = self.readers.setdefault(b, {})
            if r.get(k, 0) < v:
                r[k] = v
        for b in writes:
            self.lastw[b] = ev
            self.readers[b] = {}

    def barrier(self):
        snap = dict(self.dma_cnt)
        for e in self.ENG:
            if self.cnt[e] > 0:
                snap[e] = self.cnt[e]
        for e in self.ENG:
            self.pending[e] = dict(snap)

    def _merge_pending(self, eng, w):
        pend = self.pending.get(eng)
        if pend:
            for k, v in pend.items():
                if k == eng:
                    continue
                if self.waited[eng].get(k, 0) < v:
                    self.waited[eng][k] = v
                    w = [x for x in w if x[0] != k] + [(k, v)]
            self.pending[eng] = None
        return w

    def op(self, eng, fn, reads=(), writes=()):
        w = self._merge_pending(eng, self._waits(eng, reads, writes))
        self.cnt[eng] += 1
        ev = (eng, self.cnt[eng])
        self.ops[eng].append((w, fn, (eng, 1)))
        self._record(ev, reads, writes)
        return ev

    def dma(self, q, out_ap, in_ap, key, reads=(), writes=(), is_output=False):
        w = self._merge_pending(q, self._waits(q, reads, writes))
        self.dma_cnt[key] = self.dma_cnt.get(key, 0) + 16
        ev = (key, self.dma_cnt[key])
        self.ops[q].append((w, lambda e: e.dma_start(out=out_ap, in_=in_ap), (key, 16)))
        self._record(ev, reads, writes)
        if is_output:
            self.out_events.append(ev)
        return ev

    def emit(self):
        nc = self.nc
        fin = {}
        for k, v in self.out_events:
            fin[k] = max(fin.get(k, 0), v)
        for e in self.ENG:
            if e != "sp" and self.cnt[e] > 0:
                fin[e] = self.cnt[e]
        finw = [(k, v) for k, v in fin.items()]
        keys = set(self.dma_cnt.keys()) | set(self.ENG)
        with ExitStack() as st:
            sems = {k: st.enter_context(nc.semaphore("s_" + str(k))) for k in sorted(keys)}
            block = st.enter_context(nc.Block())

            def run(e, name):
                for w, fn, inc in self.ops[name]:
                    for k, v in w:
                        e.wait_ge(sems[k], v)
                    ins = fn(e)
                    ins.then_inc(sems[inc[0]], inc[1])
                if name == "sp":
                    for k, v in finw:
                        e.wait_ge(sems[k], v)

            block.tensor(lambda e: run(e, "pe"))
            block.scalar(lambda e: run(e, "act"))
            block.vector(lambda e: run(e, "dve"))
            block.gpsimd(lambda e: run(e, "pool"))
            block.sync(lambda e: run(e, "sp"))


class Ctx:
    def __init__(self, nc, st):
        self.nc = nc
        self.st = st
        self.n = 0

    def sb(self, shape, dt, name=None):
        self.n += 1
        return self.st.enter_context(self.nc.sbuf_tensor(name or f"t{self.n}", list(shape), dt))

    def ps(self, shape, dt=F32, name=None):
        self.n += 1
        return self.st.enter_context(self.nc.psum_tensor(name or f"p{self.n}", list(shape), dt))


class Ring:
    def __init__(self, items):
        self.items = items
        self.i = 0

    def next(self):
        it = self.items[self.i % len(self.items)]
        self.i += 1
        return it


def mm_group(P, out_ap, pairs, reads, writes):
    def fn(e):
        ins = None
        n = len(pairs)
        for i, (l, r) in enumerate(pairs):
            ins = e.matmul(out_ap, lhsT=l, rhs=r, start=(i == 0), stop=(i == n - 1))
        return ins
    return P.op("pe", fn, reads=reads, writes=writes)


class TokCommon:
    def __init__(self, nc, P, C):
        self.nc, self.P, self.C = nc, P, C
        self.x = C.sb([128, KC, TOK], F32, "x_sb")
        self.h = C.sb([128, KC, TOK], BF16, "h_sb")
        self.ones = C.sb([128, 128], BF16, "ones_bf")
        self.sq = [C.sb([128, 512], BF16, f"sq{i}") for i in range(2)]
        self.rstd = C.sb([128, 512], F32, "rstd")
        self.gs = C.sb([128, 8 * KC], F32, "gs")
        self.ps_ss = C.ps([128, 512], F32, "ps_ss")
        self.sq_ring = Ring([0, 1])
        self.cst = C.sb([128, 2], F32, "cst")
        P.op("dve", lambda e: e.memset(self.ones[:], 1.0), writes=["ones"])
        P.op("dve", lambda e: e.memset(self.cst[:, 0:1], EPS), writes=["cst"])
        P.op("dve", lambda e: e.memset(self.cst[:, 1:2], 1.0), writes=["cst"])

    def load_gain(self, g_ap, slot, ncols=KC, scale=None):
        P = self.P
        dst = self.gs[:, slot * KC: slot * KC + ncols]
        P.dma("sp", dst, g_ap, key=f"ld_g{slot}", writes=[("gs", slot)])

    def rmsnorm(self, src, src_key, dst, dst_key, slot, nch=KC, ntok_tiles=NT, tw=512, dst_c0=0):
        P = self.P
        width = nch * 128
        for t in range(ntok_tiles):
            ts = slice(t * tw, (t + 1) * tw)
            for c in range(nch):
                i = self.sq_ring.next()
                sq = self.sq[i]
                P.op("act", lambda e, sq=sq, c=c, ts=ts: e.activation(out=sq[:, 0:tw], in_=src[:, c, ts], func=AF.Square),
                     reads=[(src_key, c, t)], writes=[("sq", i)])
                last = (c == nch - 1)
                P.op("pe", lambda e, sq=sq, c=c, last=last: e.matmul(
                    self.ps_ss[:, 0:tw], lhsT=self.ones[:], rhs=sq[:, 0:tw], start=(c == 0), stop=last),
                    reads=[("sq", i), "ones"], writes=["ps_ss"])
            P.op("act", lambda e: e.activation(out=self.rstd[:, 0:tw], in_=self.ps_ss[:, 0:tw], func=AF.Sqrt,
                                               bias=self.cst[:, 0:1], scale=1.0 / width),
                 reads=["ps_ss", "cst"], writes=["rstd"])
            P.op("dve", lambda e: e.reciprocal(self.rstd[:, 0:tw], self.rstd[:, 0:tw]), reads=["rstd"], writes=["rstd"])
            for c in range(nch):
                P.op("dve", lambda e, c=c, ts=ts: e.scalar_tensor_tensor(
                    out=dst[:, dst_c0 + c, ts], in0=src[:, c, ts], scalar=self.gs[:, slot * KC + c: slot * KC + c + 1],
                    in1=self.rstd[:, 0:tw], op0=ALU.mult, op1=ALU.mult),
                    reads=[(src_key, c, t), ("gs", slot), "rstd"], writes=[(dst_key, dst_c0 + c, t)])


class FFN:
    SG = 6

    def __init__(self, T, C):
        self.T = T
        P = T.P
        self.a = C.sb([128, self.SG, TOK], BF16, "a_sb")
        self.wgu = [C.sb([128, KC, 2, 256], BF16, f"wgu{i}") for i in range(3)]
        self.wd = [C.sb([128, D], BF16, f"wd{i}") for i in range(self.SG)]
        self.sil = [C.sb([128, 512], F32, f"sil{i}") for i in range(2)]
        self.psg = [C.ps([128, 512], F32, f"psg{i}") for i in range(2)]
        self.psu = [C.ps([128, 512], F32, f"psu{i}") for i in range(2)]
        self.psd = [C.ps([128, 512], F32, f"psd{i}") for i in range(2)]
        self.wgu_i = 0
        self.pair_i = 0
        self.psd_i = 0

    def run(self, wgu_ap, wd_ap):
        T, P = self.T, self.T.P
        wgu_v = wgu_ap.rearrange("(kc p) f -> p kc f", p=128)
        sgs = []
        j = 0
        while j < NFF:
            n = min(self.SG, NFF - j)
            sgs.append((j, n))
            j += n
        for (j0, n) in sgs:
            for jj in range(n):
                P.dma("pool", self.wd[jj][:], wd_ap[(j0 + jj) * 128:(j0 + jj + 1) * 128, :],
                      key=f"ld_wd{jj}", writes=[("wd", jj)])
            for lg in range(0, n, 2):
                wi = self.wgu_i % 3
                self.wgu_i += 1
                wt = self.wgu[wi]
                c0 = (j0 + lg) * 128
                P.dma("pool", wt[:, :, 0, :], wgu_v[:, :, c0:c0 + 256], key=f"ld_wg{wi}",
                      writes=[("wgu", wi, 0)])
                P.dma("pool", wt[:, :, 1, :], wgu_v[:, :, DFF + c0:DFF + c0 + 256], key=f"ld_wu{wi}",
                      writes=[("wgu", wi, 1)])
                for jl in range(2):
                    jj = lg + jl
                    for t in range(NT):
                        ts = slice(t * 512, (t + 1) * 512)
                        pi = self.pair_i % 2
                        self.pair_i += 1
                        psg, psu, sil = self.psg[pi], self.psu[pi], self.sil[pi]
                        mm_group(P, psg[:], [(wt[:, k, 0, jl * 128:(jl + 1) * 128], T.h[:, k, ts]) for k in range(KC)],
                                 reads=[("wgu", wi, 0)] + [("h", k, t) for k in range(KC)], writes=[("psg", pi)])
                        mm_group(P, psu[:], [(wt[:, k, 1, jl * 128:(jl + 1) * 128], T.h[:, k, ts]) for k in range(KC)],
                                 reads=[("wgu", wi, 1)] + [("h", k, t) for k in range(KC)], writes=[("psu", pi)])
                        P.op("act", lambda e, sil=sil, psg=psg: e.activation(out=sil[:], in_=psg[:], func=AF.Silu),
                             reads=[("psg", pi)], writes=[("sil", pi)])
                        P.op("dve", lambda e, sil=sil, psu=psu, jj=jj, ts=ts: e.tensor_tensor(
                            self.a[:, jj, ts], sil[:], psu[:], ALU.mult),
                            reads=[("sil", pi), ("psu", pi)], writes=[("a", jj, t)])
            for c in range(KC):
                for t in range(NT):
                    ts = slice(t * 512, (t + 1) * 512)
                    di = self.psd_i % 2
                    self.psd_i += 1
                    psd = self.psd[di]
                    mm_group(P, psd[:], [(self.wd[jj][:, c * 128:(c + 1) * 128], self.a[:, jj, ts]) for jj in range(n)],
                             reads=[("wd", jj) for jj in range(n)] + [("a", jj, t) for jj in range(n)],
                             writes=[("psd", di)])
                    P.op("dve", lambda e, psd=psd, c=c, ts=ts: e.scalar_tensor_tensor(
                        out=T.x[:, c, ts], in0=psd[:], scalar=0.5, in1=T.x[:, c, ts], op0=ALU.mult, op1=ALU.add),
                        reads=[("psd", di), ("x", c, t)], writes=[("x", c, t)])


def load_x(T, xT_ap):
    for c in range(KC):
        T.P.dma("sp", T.x[:, c, :], xT_ap[c * 128:(c + 1) * 128, :], key=f"ld_x{c}",
                writes=[("x", c, t) for t in range(NT)])


def store_x(T, out_ap, src=None, key="x"):
    src = T.x if src is None else src
    for c in range(KC):
        T.P.dma("sp", out_ap[c * 128:(c + 1) * 128, :], src[:, c, :], key=f"st_x{c}",
                reads=[(key, c, t) for t in range(NT)], is_output=True)


def build_A():
    nc = bass.Bass("TRN2", target_bir_lowering=False)
    dt = nc.dram_tensor
    xT = dt("xT", [D, TOK], F32, kind="ExternalInput").ap()
    g1 = dt("g1", [128, KC], F32, kind="ExternalInput").ap()
    wgu = dt("wgu", [D, 2 * DFF], F32, kind="ExternalInput").ap()
    wd = dt("wd", [DFF, D], F32, kind="ExternalInput").ap()
    gm = dt("gm", [128, KC], F32, kind="ExternalInput").ap()
    win = dt("win", [D, IN_COLS], F32, kind="ExternalInput").ap()
    bfn = dt("bf", [NH, 1], F32, kind="ExternalInput").ap()
    x1T = dt("x1T", [D, TOK], F32, kind="ExternalOutput").ap()
    uT = dt("uT", [512, TOK], F32, kind="ExternalOutput").ap()
    zbT = dt("zbT", [512, TOK], F32, kind="ExternalOutput").ap()
    qT = dt("qT", [512, TOK], BF16, kind="ExternalOutput").ap()
    kT = dt("kT", [512, TOK], BF16, kind="ExternalOutput").ap()
    vT = dt("vT", [512, TOK], BF16, kind="ExternalOutput").ap()
    lf = dt("lf", [NH, TOK], F32, kind="ExternalOutput").ap()

    P = Prog(nc)
    with ExitStack() as st:
        C = Ctx(nc, st)
        T = TokCommon(nc, P, C)
        ffn = FFN(T, C)
        load_x(T, xT)
        T.load_gain(g1, 0)
        T.load_gain(gm, 1)
        T.rmsnorm(T.x, "x", T.h, "h", 0)
        ffn.run(wgu, wd)
        store_x(T, x1T)
        T.rmsnorm(T.x, "x", T.h, "h", 1)
        mixin(T, C, ffn, win, bfn, uT, zbT, qT, kT, vT, lf)
        P.emit()
    return nc


def mixin(T, C, ffn, win, bfn, uT, zbT, qT, kT, vT, lf):
    P = T.P
    win_v = win.rearrange("(kc p) f -> p kc f", p=128)
    wz = [w[:].rearrange("p k a b -> p k (a b)") for w in ffn.wgu]
    wf = C.sb([128, KC, NH], BF16, "wf")
    stf = [C.sb([128, 512], F32, f"stf{i}") for i in range(4)]
    stb = [C.sb([128, 512], BF16, f"stb{i}") for i in range(4)]
    nb = C.sb([NH, 1], F32, "negb")
    lft = C.sb([NH, 512], F32, "lft")
    ex = C.sb([NH, 512], F32, "ex")
    stf_i = [0]
    stb_i = [0]
    wz_i = [ffn.wgu_i]
    pair_i = [0]

    def load_group(g):
        wi = wz_i[0] % 3
        wz_i[0] += 1
        P.dma("pool", wz[wi], win_v[:, :, g * 512:(g + 1) * 512], key=f"ld_wg{wi}",
              writes=[("wgu", wi, 0), ("wgu", wi, 1)])
        return wi

    def proj(wi, i, t, ps, pskey):
        ts = slice(t * 512, (t + 1) * 512)
        mm_group(P, ps[:], [(wz[wi][:, k, i * 128:(i + 1) * 128], T.h[:, k, ts]) for k in range(KC)],
                 reads=[("wgu", wi, 0), ("wgu", wi, 1)] + [("h", k, t) for k in range(KC)], writes=[pskey])

    def evac(ps, pskey, out_dram, i, t, dtype, scale=1.0):
        ts = slice(t * 512, (t + 1) * 512)
        if dtype == "f32":
            si = stf_i[0] % 4
            stf_i[0] += 1
            stt, sk = stf[si], ("stf", si)
        else:
            si = stb_i[0] % 4
            stb_i[0] += 1
            stt, sk = stb[si], ("stb", si)
        P.op("act", lambda e: e.activation(out=stt[:], in_=ps[:], func=AF.Copy, scale=float(scale)),
             reads=[pskey], writes=[sk])
        P.dma("sp", out_dram[i * 128:(i + 1) * 128, ts], stt[:], key=f"st_{sk[0]}{si}", reads=[sk], is_output=True)

    wi = load_group(0)
    for i in range(4):
        for t in range(NT):
            pi = pair_i[0] % 2
            pair_i[0] += 1
            proj(wi, i, t, ffn.psg[pi], ("psg", pi))
            evac(ffn.psg[pi], ("psg", pi), zbT, i, t, "f32")
    wc = load_group(1)
    wv = load_group(2)
    for i in range(4):
        for t in range(NT):
            ts = slice(t * 512, (t + 1) * 512)
            pi = pair_i[0] % 2
            pair_i[0] += 1
            psg, psu, sil = ffn.psg[pi], ffn.psu[pi], ffn.sil[pi]
            proj(wc, i, t, psg, ("psg", pi))
            proj(wv, i, t, psu, ("psu", pi))
            P.op("act", lambda e, sil=sil, psg=psg: e.activation(out=sil[:], in_=psg[:], func=AF.Copy),
                 reads=[("psg", pi)], writes=[("sil", pi)])
            si = stf_i[0] % 4
            stf_i[0] += 1
            stt = stf[si]
            P.op("dve", lambda e, stt=stt, sil=sil, psu=psu: e.tensor_tensor(stt[:], sil[:], psu[:], ALU.mult),
                 reads=[("sil", pi), ("psu", pi)], writes=[("stf", si)])
            P.dma("sp", uT[i * 128:(i + 1) * 128, ts], stt[:], key=f"st_stf{si}", reads=[("stf", si)], is_output=True)
    for g, dram, sc in ((3, qT, 0.125), (4, kT, 1.0), (5, vT, 1.0)):
        wi = load_group(g)
        for i in range(4):
            for t in range(NT):
                pi = pair_i[0] % 2
                pair_i[0] += 1
                proj(wi, i, t, ffn.psg[pi], ("psg", pi))
                evac(ffn.psg[pi], ("psg", pi), dram, i, t, "bf16", sc)
    P.dma("pool", wf[:], win_v[:, :, 3072:3080], key="ld_wf", writes=["wf"])
    P.dma("sp", nb[:], bfn, key="ld_bf", writes=["negb"])
    P.op("dve", lambda e: e.tensor_scalar(nb[:], nb[:], -1.0, None, ALU.mult), reads=["negb"], writes=["negb"])
    for t in range(NT):
        ts = slice(t * 512, (t + 1) * 512)
        pi = pair_i[0] % 2
        pair_i[0] += 1
        ps = ffn.psg[pi]
        mm_group(P, ps[0:NH, :], [(wf[:, k, :], T.h[:, k, ts]) for k in range(KC)],
                 reads=["wf"] + [("h", k, t) for k in range(KC)], writes=[("psg", pi)])
        P.op("act", lambda e, ps=ps: e.activation(out=ex[:], in_=ps[0:NH, :], func=AF.Exp, bias=nb[:], scale=-1.0),
             reads=[("psg", pi), "negb"], writes=["ex"])
        P.op("act", lambda e: e.activation(out=lft[:], in_=ex[:], func=AF.Ln, bias=T.cst[0:NH, 1:2], scale=1.0),
             reads=["ex", "cst"], writes=["lft"])
        P.op("dve", lambda e: e.tensor_scalar(lft[:], lft[:], -1.0, None, ALU.mult),
             reads=["lft"], writes=["lft"])
        P.dma("sp", lf[:, ts], lft[:], key="st_lf", reads=["lft"], is_output=True)


_CACHE = {}


def _prog(name, builder):
    if name not in _CACHE:
        _CACHE[name] = builder()
    return _CACHE[name]


def _gl(g):
    return np.ascontiguousarray(np.asarray(g, np.float32).reshape(-1, 128).T)


def run_A(xT_cores, g1, wgu, wd, gm, win, bf):
    nc = _prog("A", build_A)
    in_maps = []
    for c in range(NCORES):
        in_maps.append({
            "xT": xT_cores[c], "g1": _gl(g1), "wgu": wgu, "wd": wd, "gm": _gl(gm), "win": win,
            "bf": np.ascontiguousarray(np.asarray(bf, np.float32).reshape(NH, 1)),
        })
    res = run_bass_kernel_spmd(nc, in_maps, core_ids=list(range(NCORES)))
    return res.results


NBLK = S // 128
NCH = S // 512


def build_B():
    nc = bass.Bass("TRN2", target_bir_lowering=False)
    dt = nc.dram_tensor
    qTh = dt("qTh", [HD, S], BF16, kind="ExternalInput").ap()
    kTh = dt("kTh", [HD, S], BF16, kind="ExternalInput").ap()
    vh = dt("vh", [128, NBLK * HD], BF16, kind="ExternalInput").ap()
    lfh = dt("lfh", [NBLK, 128], F32, kind="ExternalInput").ap()
    cU = dt("cU", [128, 128], F32, kind="ExternalInput").ap()
    cW = dt("cW", [128, 128], F32, kind="ExternalInput").ap()
    cS = dt("cS", [128, NCH], F32, kind="ExternalInput").ap()
    cI = dt("cI", [128, 128], F32, kind="ExternalInput").ap()
    yT = dt("yT", [HD, S], F32, kind="ExternalOutput").ap()
    a_scr = dt("a_scr", [NBLK, 128], BF16).ap()

    P = Prog(nc)
    with ExitStack() as st:
        C = Ctx(nc, st)
        qa = C.sb([HD + 1, S], BF16, "qa")
        ka = C.sb([HD + 1, S], BF16, "ka")
        va = C.sb([128, NBLK, HD + 1], BF16, "va")
        vs = C.sb([128, NBLK * HD], BF16, "vs")
        lf = C.sb([128, 128], F32, "lf")
        lc = C.sb([128, 128], F32, "lc")
        cg = C.sb([128, 128], F32, "cg")
        cT = C.sb([128, 128], F32, "cT")
        abf = C.sb([128, 128], BF16, "abf")
        U = C.sb([128, 128], F32, "U")
        W = C.sb([128, 128], F32, "W")
        Sst = C.sb([128, NCH], F32, "Sst")
        I = C.sb([128, 128], F32, "I")
        Tb = C.sb([128, 128], F32, "Tb")
        onesf = C.sb([128, 128], F32, "onesf")
        cd = C.sb([128, 2], F32, "cd")
        crefs = C.sb([128, NCH], F32, "crefs")
        bias = [C.sb([128, NBLK], F32, f"bias{i}") for i in range(2)]
        pT = [C.sb([128, 512], BF16, f"pT{i}") for i in range(4)]
        rec = C.sb([HD + 1, 512], F32, "rec")
        bcs = C.sb([HD, 512], F32, "bcs")
        yo = [C.sb([HD, 512], F32, f"yo{i}") for i in range(2)]
        ps_s = [C.ps([128, 512], F32, f"ps_s{i}") for i in range(4)]
        ps_o = [C.ps([128, 512], F32, f"ps_o{i}") for i in range(2)]
        ps_b = C.ps([128, 512], F32, "ps_b")
        ps_m = C.ps([128, 512], F32, "ps_m")

        for i, (t, src) in enumerate(((lf, lfh), (U, cU), (W, cW), (Sst, cS), (I, cI))):
            P.dma("sp", t[:], src, key=f"ld_c{i}", writes=[t.name])
        nq = 4
        for i in range(nq):
            cs = slice(i * (S // nq), (i + 1) * (S // nq))
            P.dma("sp", ka[0:HD, cs], kTh[:, cs], key=f"ld_k{i}", writes=[("ka", i)])
            P.dma("sp", qa[0:HD, cs], qTh[:, cs], key=f"ld_q{i}", writes=[("qa", i)])
        P.op("dve", lambda e: e.memset(ka[HD:HD + 1, :], 1.0), writes=["ka_aug"])
        P.op("dve", lambda e: e.memset(onesf[:], 1.0), writes=["onesf"])
        P.op("pool", lambda e: e.memset(va[:], 1.0), writes=[("va", i) for i in range(4)])
        vs_v = vs[:].rearrange("p (b d) -> p b d", d=HD)
        for i in range(4):
            bs = slice(i * 32, (i + 1) * 32)
            P.dma("sp", vs[:, i * 32 * HD:(i + 1) * 32 * HD], vh[:, i * 32 * HD:(i + 1) * 32 * HD],
                  key=f"ld_v{i}", writes=[("vs", i)])
            P.op("pool", lambda e, bs=bs: e.tensor_copy(va[:, bs, 0:HD], vs_v[:, bs, :]),
                 reads=[("vs", i)], writes=[("va", i)])

        P.op("dve", lambda e: e.tensor_tensor_scan(lc[:], lf[:], lf[:], 0.0, ALU.add, ALU.bypass),
             reads=["lf"], writes=["lc"])
        P.op("dve", lambda e: e.tensor_scalar(Tb[:], onesf[:], lc[:, 127:128], None, ALU.mult),
             reads=["lc", "onesf"], writes=["Tb"])
        P.op("pe", lambda e: e.matmul(ps_m[:, 0:1], lhsT=U[:], rhs=lc[:, 127:128], start=True, stop=True),
             reads=["U", "lc"], writes=["ps_m0"])
        P.op("pe", lambda e: e.matmul(ps_m[:, 1:2], lhsT=W[:], rhs=lc[:, 127:128], start=True, stop=True),
             reads=["W", "lc"], writes=["ps_m1"])
        P.op("pe", lambda e: e.matmul(ps_m[:, 64:64 + NCH], lhsT=Tb[:], rhs=Sst[:], start=True, stop=True),
             reads=["Tb", "Sst"], writes=["ps_m2"])
        P.op("act", lambda e: e.activation(out=cd[:], in_=ps_m[:, 0:2], func=AF.Copy),
             reads=["ps_m0", "ps_m1"], writes=["cd"])
        P.op("act", lambda e: e.activation(out=crefs[:], in_=ps_m[:, 64:64 + NCH], func=AF.Copy),
             reads=["ps_m2"], writes=["crefs"])
        P.op("dve", lambda e: e.tensor_scalar(cg[:], lc[:], cd[:, 0:1], None, ALU.add),
             reads=["lc", "cd"], writes=["cg"])
        P.op("dve", lambda e: e.tensor_scalar(abf[:], lc[:], cd[:, 1:2], None, ALU.add),
             reads=["lc", "cd"], writes=["abf"])
        P.op("pe", lambda e: e.transpose(ps_b[:, 0:128], cg[:], I[:]), reads=["cg", "I"], writes=["ps_b"])
        P.op("act", lambda e: e.activation(out=cT[:], in_=ps_b[:, 0:128], func=AF.Copy),
             reads=["ps_b"], writes=["cT"])
        P.dma("sp", a_scr, abf[:], key="st_a", reads=["abf"], writes=["a_scr"])
        P.dma("sp", qa[HD:HD + 1, :], a_scr.rearrange("(o b) i -> o (b i)", o=1), key="ld_a",
              reads=["a_scr"], writes=["qa_aug"])

        LA = 2
        items = []
        for n in range(NCH):
            nj = 4 * n + 4
            for j in range(nj):
                items.append((n, j, nj))

        def issue_bias(n):
            nj = 4 * n + 4
            bt = bias[n % 2]
            P.op("dve", lambda e, bt=bt, nj=nj, n=n: e.tensor_scalar(
                bt[:, 0:nj], cT[:, 0:nj], -1.0, crefs[:, n:n + 1], ALU.mult, ALU.add),
                reads=["cT", "crefs"], writes=[("bias", n % 2)])

        def issue_S(idx):
            n, j, nj = items[idx]
            if j == 0:
                issue_bias(n)
            bt = bias[n % 2]
            q0 = max(0, j - 4 * n) * 128
            si = idx % 4
            ps, pt = ps_s[si], pT[si]
            qi = (n * 512) // (S // 4)
            ki = (j * 128) // (S // 4)
            P.op("pe", lambda e: e.matmul(
                ps[:, q0:512], lhsT=ka[:, j * 128:(j + 1) * 128], rhs=qa[:, n * 512 + q0:(n + 1) * 512],
                start=True, stop=True),
                reads=[("ka", ki), "ka_aug", ("qa", qi), "qa_aug"], writes=[("ps_s", si)])
            P.op("act", lambda e: e.activation(
                out=pt[:, q0:512], in_=ps[:, q0:512], func=AF.Exp, bias=bt[:, j:j + 1], scale=1.0),
                reads=[("ps_s", si), ("bias", n % 2)], writes=[("pT", si)])
            if j - 4 * n >= 0:
                P.op("pool", lambda e: e.affine_select(
                    out=pt[:, q0:q0 + 128], in_=pt[:, q0:q0 + 128], pattern=[[1, 128]],
                    compare_op=ALU.is_ge, fill=0.0, base=0, channel_multiplier=-1),
                    reads=[("pT", si)], writes=[("pT", si)])

        def issue_PV(idx):
            n, j, nj = items[idx]
            q0 = max(0, j - 4 * n) * 128
            si = idx % 4
            pt = pT[si]
            po = ps_o[n % 2]
            P.op("pe", lambda e: e.matmul(
                po[0:HD + 1, q0:512], lhsT=va[:, j, :], rhs=pt[:, q0:512],
                start=(j == 0), stop=(j == nj - 1)),
                reads=[("pT", si), ("va", j // 32)], writes=[("ps_o", n % 2)])

        def norm_pre(n):
            po = ps_o[n % 2]
            P.op("dve", lambda e: e.reciprocal(rec[HD:HD + 1, :], po[HD:HD + 1, :]),
                 reads=[("ps_o", n % 2)], writes=["rec"])

        def norm_post(n):
            po = ps_o[n % 2]
            P.op("pe", lambda e: e.matmul(ps_b[0:HD, :], lhsT=onesf[HD:HD + 1, 0:HD], rhs=rec[HD:HD + 1, :],
                                          start=True, stop=True),
                 reads=["rec", "onesf"], writes=["ps_b"])
            P.op("act", lambda e: e.activation(out=bcs[:], in_=ps_b[0:HD, :], func=AF.Copy),
                 reads=["ps_b"], writes=["bcs"])
            yt = yo[n % 2]
            P.op("dve", lambda e: e.tensor_tensor(yt[:], po[0:HD, :], bcs[:], ALU.mult),
                 reads=[("ps_o", n % 2), "bcs"], writes=[("yo", n % 2)])
            P.dma("sp", yT[:, n * 512:(n + 1) * 512], yt[:], key=f"st_y{n % 2}", reads=[("yo", n % 2)],
                  is_output=True)

        NI = len(items)
        pend_norm = []
        for idx in range(NI + LA):
            if idx < NI:
                issue_S(idx)
            k = idx - LA
            if k >= 0:
                issue_PV(k)
                n, j, nj = items[k]
                if j == nj - 1:
                    norm_pre(n)
                    pend_norm.append((idx + 3, n))
            while pend_norm and pend_norm[0][0] <= idx:
                norm_post(pend_norm.pop(0)[1])
        for _, n in pend_norm:
            norm_post(n)
        P.emit()
    return nc


def b_consts():
    bp = np.arange(128)[:, None]
    bk = np.arange(128)[None, :]
    U = (bp < bk).astype(np.float32)
    W = ((bp < bk) & (bp // 4 == bk // 4)).astype(np.float32)
    Sst = (bp < 4 * np.arange(NCH)[None, :]).astype(np.float32)
    I = np.eye(128, dtype=np.float32)
    return {"cU": U, "cW": W, "cS": Sst, "cI": I}


def run_B(qT_heads, kT_heads, v_heads, lf_heads):
    nc = _prog("B", build_B)
    cst = b_consts()
    in_maps = []
    for h in range(NH):
        vh = np.ascontiguousarray(v_heads[h].reshape(NBLK, 128, HD).transpose(1, 0, 2).reshape(128, NBLK * HD))
        m = {"qTh": np.ascontiguousarray(qT_heads[h]), "kTh": np.ascontiguousarray(kT_heads[h]), "vh": vh,
             "lfh": np.ascontiguousarray(lf_heads[h].reshape(NBLK, 128))}
        m.update(cst)
        in_maps.append(m)
    res = run_bass_kernel_spmd(nc, in_maps, core_ids=list(range(NCORES)))
    return res.results


NXH = 4


def build_C(final):
    nc = bass.Bass("TRN2", target_bir_lowering=False)
    dt = nc.dram_tensor
    inp = lambda n, s, d=F32: dt(n, s, d, kind="ExternalInput").ap()
    x1T = inp("x1T", [D, TOK])
    uTh = inp("uTh", [512, TOK + 2])
    zbT = inp("zbT", [512, TOK])
    yaT = inp("yaT", [512, TOK])
    wcv = inp("wcv", [128, 12])
    gco = inp("gco", [128, 4])
    gao = inp("gao", [128, 4])
    wmo = inp("wmo", [D, D])
    gx = inp("gx", [128, KC])
    gme = inp("gme", [128, KC])
    memT = inp("memT", [D, NMEM])
    wxq = inp("wxq", [D, D])
    wxkv = inp("wxkv", [D, 2 * D])
    wxo = inp("wxo", [D, D])
    g2 = inp("g2", [128, KC])
    wgu = inp("wgu", [D, 2 * DFF])
    wd = inp("wd", [DFF, D])
    gf = inp("gf", [128, KC]) if final else None
    outT = dt("outT", [D, TOK], F32, kind="ExternalOutput").ap()

    P = Prog(nc)
    with ExitStack() as st:
        C = Ctx(nc, st)
        T = TokCommon(nc, P, C)
        wc = C.sb([128, 12], F32, "wc")
        load_x(T, x1T)
        P.dma("sp", wc[:], wcv, key="ld_wc", writes=["wc"])
        T.load_gain(gco, 0, ncols=4)
        T.load_gain(gao, 1, ncols=4)
        T.load_gain(gx, 2)
        T.load_gain(gme, 3)
        T.load_gain(g2, 4)
        if final:
            T.load_gain(gf, 5)

        with ExitStack() as s1:
            C1 = Ctx(nc, s1)
            C1.n = 100
            yc = C1.sb([128, 4, TOK], F32, "yc")
            ycat = C1.sb([128, KC, TOK], BF16, "ycat")
            ut = C1.sb([128, TOK + 2], F32, "ut")
            zt = C1.sb([128, TOK], F32, "zt")
            wb = C1.sb([128, KC, D], BF16, "wb0")
            psp = [C1.ps([128, 512], F32, f"psp{i}") for i in range(2)]
            P.dma("pool", wb[:], wmo.rearrange("(kc p) f -> p kc f", p=128), key="ld_wb0", writes=["wb0"])
            for ci in range(4):
                P.dma("sp", ut[:], uTh[ci * 128:(ci + 1) * 128, :], key="ld_ut", writes=["ut"])
                P.dma("sp", zt[:], zbT[ci * 128:(ci + 1) * 128, :], key="ld_zt", writes=["zt"])
                ykeys = [("yc", ci, t) for t in range(NT)]
                P.op("dve", lambda e, ci=ci: e.tensor_scalar(yc[:, ci, :], ut[:, 0:TOK], wc[:, ci * 3:ci * 3 + 1], None, ALU.mult),
                     reads=["ut", "wc"], writes=ykeys)
                for k in (1, 2):
                    P.op("dve", lambda e, ci=ci, k=k: e.scalar_tensor_tensor(
                        out=yc[:, ci, :], in0=ut[:, k:k + TOK], scalar=wc[:, ci * 3 + k:ci * 3 + k + 1],
                        in1=yc[:, ci, :], op0=ALU.mult, op1=ALU.add),
                        reads=["ut", "wc"] + ykeys, writes=ykeys)
                P.op("pool", lambda e, ci=ci: e.tensor_tensor(yc[:, ci, :], yc[:, ci, :], zt[:], ALU.mult),
                     reads=["zt"] + ykeys, writes=ykeys)
            T.rmsnorm(yc, "yc", ycat, "ycat", 0, nch=4, dst_c0=0)
            for ci in range(4):
                P.dma("sp", yc[:, ci, :], yaT[ci * 128:(ci + 1) * 128, :], key=f"ld_ya{ci}",
                      writes=[("yc", ci, t) for t in range(NT)])
            T.rmsnorm(yc, "yc", ycat, "ycat", 1, nch=4, dst_c0=4)
            pi = 0
            for c in range(KC):
                for t in range(NT):
                    ts = slice(t * 512, (t + 1) * 512)
                    ps = psp[pi % 2]
                    pk = ("psp", pi % 2)
                    pi += 1
                    mm_group(P, ps[:], [(wb[:, k, c * 128:(c + 1) * 128], ycat[:, k, ts]) for k in range(KC)],
                             reads=["wb0"] + [("ycat", k, t) for k in range(KC)], writes=[pk])
                    P.op("dve", lambda e, ps=ps, c=c, ts=ts: e.tensor_tensor(T.x[:, c, ts], ps[:], T.x[:, c, ts], ALU.add),
                         reads=[pk, ("x", c, t)], writes=[("x", c, t)])
        P.barrier()

        with ExitStack() as s2:
            C2 = Ctx(nc, s2)
            C2.n = 200
            wbs = [C2.sb([128, KC, D], BF16, f"wbx{i}") for i in range(2)]
            mt = C2.sb([128, KC, NMEM], F32, "mt")
            mn = C2.sb([128, KC, NMEM], BF16, "mn")
            kmT = C2.sb([128, KC, NMEM], BF16, "kmT")
            vm = C2.sb([128, 2, D], BF16, "vm")
            qx = C2.sb([128, KC, 512], BF16, "qx")
            oT = C2.sb([128, KC, 512], BF16, "oT")
            pm = [C2.sb([128, 512], BF16, f"pm{i}") for i in range(2)]
            rden = C2.sb([128, 512], F32, "rden")
            psp = [C2.ps([128, 512], F32, f"psq{i}") for i in range(2)]
            pss = [C2.ps([128, 512], F32, f"pss{i}") for i in range(2)]
            pso = [C2.ps([128, 512], F32, f"pso{i}") for i in range(2)]
            psd = C2.ps([128, 512], F32, "psden")
            wv = lambda w: w.rearrange("(kc p) f -> p kc f", p=128)
            P.dma("pool", wbs[0][:], wv(wxkv)[:, :, 0:D], key="ld_wbx0", writes=["wbx0"])
            P.dma("pool", wbs[1][:], wv(wxkv)[:, :, D:2 * D], key="ld_wbx1", writes=["wbx1"])
            P.dma("sp", mt[:], memT.rearrange("(kc p) m -> p kc m", p=128), key="ld_mt",
                  writes=[("mt", c, 0) for c in range(KC)])
            T.rmsnorm(mt, "mt", mn, "mn", 3, nch=KC, ntok_tiles=1, tw=NMEM)
            mnk = [("mn", c, 0) for c in range(KC)]
            pi = 0
            for fc in range(KC):
                ps, pk = psp[pi % 2], ("psq", pi % 2)
                pi += 1
                mm_group(P, ps[:, 0:NMEM], [(wbs[0][:, k, fc * 128:(fc + 1) * 128], mn[:, k, :]) for k in range(KC)],
                         reads=["wbx0"] + mnk, writes=[pk])
                P.op("act", lambda e, ps=ps, fc=fc: e.activation(out=kmT[:, fc, :], in_=ps[:, 0:NMEM], func=AF.Copy),
                     reads=[pk], writes=["kmT"])
            for mb in range(2):
                for fh in range(2):
                    ps, pk = psp[pi % 2], ("psq", pi % 2)
                    pi += 1
                    mm_group(P, ps[:], [(mn[:, k, mb * 128:(mb + 1) * 128], wbs[1][:, k, fh * 512:(fh + 1) * 512])
                                        for k in range(KC)], reads=["wbx1"] + mnk, writes=[pk])
                    P.op("act", lambda e, ps=ps, mb=mb, fh=fh: e.activation(
                        out=vm[:, mb, fh * 512:(fh + 1) * 512], in_=ps[:], func=AF.Copy), reads=[pk], writes=["vm"])
            P.dma("pool", wbs[0][:], wv(wxq), key="ld_wbx0", writes=["wbx0"])
            P.dma("pool", wbs[1][:], wv(wxo), key="ld_wbx1", writes=["wbx1"])
            T.rmsnorm(T.x, "x", T.h, "h", 2)
            si = 0
            oi = 0
            for t in range(NT):
                ts = slice(t * 512, (t + 1) * 512)
                for fc in range(KC):
                    ps, pk = psp[pi % 2], ("psq", pi % 2)
                    pi += 1
                    mm_group(P, ps[:], [(wbs[0][:, k, fc * 128:(fc + 1) * 128], T.h[:, k, ts]) for k in range(KC)],
                             reads=["wbx0"] + [("h", k, t) for k in range(KC)], writes=[pk])
                    P.op("act", lambda e, ps=ps, fc=fc: e.activation(out=qx[:, fc, :], in_=ps[:], func=AF.Copy),
                         reads=[pk], writes=[("qx", fc)])
                for hh in range(NXH):
                    for mb in range(2):
                        ps, pk = pss[si % 2], ("pss", si % 2)
                        pt, ptk = pm[mb], ("pm", mb)
                        si += 1
                        mm_group(P, ps[:], [(kmT[:, 2 * hh + dc, mb * 128:(mb + 1) * 128], qx[:, 2 * hh + dc, :])
                                            for dc in range(2)],
                                 reads=["kmT", ("qx", 2 * hh), ("qx", 2 * hh + 1)], writes=[pk])
                        P.op("act", lambda e, ps=ps, pt=pt: e.activation(out=pt[:], in_=ps[:], func=AF.Exp, scale=1.0 / 16.0),
                             reads=[pk], writes=[ptk])
                    mm_group(P, psd[:], [(T.ones[:], pm[mb][:]) for mb in range(2)],
                             reads=["ones", ("pm", 0), ("pm", 1)], writes=["psden"])
                    P.op("dve", lambda e: e.reciprocal(rden[:], psd[:]), reads=["psden"], writes=["rden"])
                    for dc in range(2):
                        fc = 2 * hh + dc
                        ps, pk = pso[oi % 2], ("pso", oi % 2)
                        oi += 1
                        mm_group(P, ps[:], [(vm[:, mb, fc * 128:(fc + 1) * 128], pm[mb][:]) for mb in range(2)],
                                 reads=["vm", ("pm", 0), ("pm", 1)], writes=[pk])
                        P.op("dve", lambda e, ps=ps, fc=fc: e.tensor_tensor(oT[:, fc, :], ps[:], rden[:], ALU.mult),
                             reads=[pk, "rden"], writes=[("oT", fc)])
                for c in range(KC):
                    ps, pk = psp[pi % 2], ("psq", pi % 2)
                    pi += 1
                    mm_group(P, ps[:], [(wbs[1][:, k, c * 128:(c + 1) * 128], oT[:, k, :]) for k in range(KC)],
                             reads=["wbx1"] + [("oT", k) for k in range(KC)], writes=[pk])
                    P.op("dve", lambda e, ps=ps, c=c, ts=ts: e.tensor_tensor(T.x[:, c, ts], ps[:], T.x[:, c, ts], ALU.add),
                         reads=[pk, ("x", c, t)], writes=[("x", c, t)])
        P.barrier()

        with ExitStack() as s3:
            C3 = Ctx(nc, s3)
            C3.n = 300
            ffn = FFN(T, C3)
            T.rmsnorm(T.x, "x", T.h, "h", 4)
            ffn.run(wgu, wd)
        if final:
            P.barrier()
            with ExitStack() as s4:
                C4 = Ctx(nc, s4)
                C4.n = 400
                of = C4.sb([128, KC, TOK], F32, "of")
                T.rmsnorm(T.x, "x", of, "of", 5)
                store_x(T, outT, src=of, key="of")
                P.emit()
        else:
            store_x(T, outT)
            P.emit()
    return nc


def run_C(final, x1T_cores, uTh_cores, zbT_cores, yaT_cores, w, l, memT):
    name = "Cf" if final else "C"
    nc = _prog(name, lambda: build_C(final))
    wcv = np.ascontiguousarray(np.asarray(w["w_conv"][l], np.float32).T.reshape(4, 128, 3).transpose(1, 0, 2).reshape(128, 12))
    in_maps = []
    for c in range(NCORES):
        m = {
            "x1T": x1T_cores[c], "uTh": uTh_cores[c], "zbT": zbT_cores[c], "yaT": yaT_cores[c],
            "wcv": wcv, "gco": _gl(w["g_conv_out"][l]), "gao": _gl(w["g_att_out"][l]), "wmo": w["w_mix_out"][l],
            "gx": _gl(w["g_xattn"][l]), "gme": _gl(w["g_mem"][l]), "memT": memT,
            "wxq": w["w_xq"][l], "wxkv": w["w_xkv"][l], "wxo": w["w_xo"][l],
            "g2": _gl(w["g_ffn2"][l]), "wgu": w["w_ffn2_gu"][l], "wd": w["w_ffn2_down"][l],
        }
        if final:
            m["gf"] = _gl(w["g_final"])
        in_maps.append(m)
    res = run_bass_kernel_spmd(nc, in_maps, core_ids=list(range(NCORES)))
    return res.results


def kernel(**inputs):
    w = {k: np.asarray(v) for k, v in inputs.items()}
    x = w["x"][0]
    memT = np.ascontiguousarray(w["mem"][0].T)
    xT = [np.ascontiguousarray(x[c * TOK:(c + 1) * TOK].T) for c in range(NCORES)]
    depth = w["g_ffn1"].shape[0]
    for l in range(depth):
        ra = run_A(xT, w["g_ffn1"][l], w["w_ffn1_gu"][l], w["w_ffn1_down"][l], w["g_mix"][l],
                   w["w_mix_in"][l], w["b_f"][l])
        cat = lambda k: np.concatenate([ra[c][k] for c in range(NCORES)], axis=1)
        qf, kf, vf, lff, uf = cat("qT"), cat("kT"), cat("vT"), cat("lf"), cat("uT")
        qh = [qf[h * HD:(h + 1) * HD] for h in range(NH)]
        kh = [kf[h * HD:(h + 1) * HD] for h in range(NH)]
        vh = [np.ascontiguousarray(vf[h * HD:(h + 1) * HD].T) for h in range(NH)]
        lh = [np.ascontiguousarray(lff[h]) for h in range(NH)]
        rb = run_B(qh, kh, vh, lh)
        ya = np.concatenate([rb[h]["yT"] for h in range(NH)], axis=0)
        up = np.concatenate([np.zeros((512, 2), np.float32), uf], axis=1)
        rc = run_C(l == depth - 1,
                   [ra[c]["x1T"] for c in range(NCORES)],
                   [np.ascontiguousarray(up[:, c * TOK:(c + 1) * TOK + 2]) for c in range(NCORES)],
                   [ra[c]["zbT"] for c in range(NCORES)],
                   [np.ascontiguousarray(ya[:, c * TOK:(c + 1) * TOK]) for c in range(NCORES)],
                   w, l, memT)
        xT = [rc[c]["outT"] for c in range(NCORES)]
    out = np.concatenate([np.asarray(xT[c], np.float32).T for c in range(NCORES)], axis=0)
    return np.ascontiguousarray(out[None]).astype(np.float32)
```
